# Optimizing a Trainium2 kernel written in Bass

```python
import math
import jax, jax.numpy as jnp
from jax import lax
import numpy as np

D_MODEL = 1024
BATCH = 16
SEQ = 2048
DEPTH = 1

ATTN_HEAD_DIM = 128
ATTN_HEADS = D_MODEL // ATTN_HEAD_DIM
D_ATTN = ATTN_HEADS * ATTN_HEAD_DIM
MOBA_BLOCK = 256
MOBA_TOPK = 3
MOBA_QCHUNK = 8
SSM_HEAD_DIM = 64
D_SSM = D_MODEL
SSM_HEADS = D_SSM // SSM_HEAD_DIM
SSM_GROUPS = 2
SSM_STATE = 128
SSM_CONV = 4
SSM_CHUNK = 256
D_XBC = D_SSM + 2 * SSM_GROUPS * SSM_STATE
D_MIX = D_ATTN + D_SSM
D_IN = 3 * D_ATTN + D_SSM + D_XBC + SSM_HEADS
D_FF = ((8 * D_MODEL // 3) + 255) // 256 * 256
FFN_CONV = 3
N_MOD = 6
EPS = 1e-6
DT_MIN = 1e-3
DT_MAX = 1e-1

kernel_name = "hybrid_ssd_moba_convffn_adaln"


def rms_norm(x, g):
    xf = x.astype(jnp.float32)
    y = xf * lax.rsqrt(jnp.mean(xf * xf, axis=-1, keepdims=True) + EPS)
    return (y * g.astype(jnp.float32)).astype(x.dtype)


def causal_dwconv(x, w, b):
    width, ch = w.shape
    y = lax.conv_general_dilated(
        x, w[:, None, :], window_strides=(1,), padding=[(width - 1, 0)],
        dimension_numbers=("NWC", "WIO", "NWC"), feature_group_count=ch)
    return y + b


def pad_seq(a, axis, mult):
    extra = (-a.shape[axis]) % mult
    if extra == 0:
        return a
    widths = [(0, 0)] * a.ndim
    widths[axis] = (0, extra)
    return jnp.pad(a, widths)


def ssd_chunked(xdt, da, bm, cm):
    bsz, seq, nh, hp = xdt.shape
    ng, ns = bm.shape[2], bm.shape[3]
    hpg = nh // ng
    xdt, da, bm, cm = (pad_seq(t.astype(jnp.float32), 1, SSM_CHUNK) for t in (xdt, da, bm, cm))
    nc = xdt.shape[1] // SSM_CHUNK
    xc = xdt.reshape(bsz, nc, SSM_CHUNK, ng, hpg, hp)
    bc = bm.reshape(bsz, nc, SSM_CHUNK, ng, ns)
    cc = cm.reshape(bsz, nc, SSM_CHUNK, ng, ns)
    a = da.reshape(bsz, nc, SSM_CHUNK, ng, hpg).transpose(0, 3, 4, 1, 2)
    a_cum = jnp.cumsum(a, axis=-1)
    causal = jnp.tril(jnp.ones((SSM_CHUNK, SSM_CHUNK), dtype=bool))
    decay_in = jnp.exp(jnp.where(causal, a_cum[..., :, None] - a_cum[..., None, :], -jnp.inf))
    cb = jnp.einsum("bclgn,bcsgn->bgcls", cc, bc)
    y_diag = jnp.einsum("bgcls,bgecls,bcsgep->bclgep", cb, decay_in, xc)
    decay_to_end = jnp.exp(a_cum[..., -1:] - a_cum)
    states = jnp.einsum("bclgn,bgecl,bclgep->bcgepn", bc, decay_to_end, xc)
    chunk_decay = jnp.exp(a_cum[..., -1])

    def carry_state(h_prev, inp):
        st, dec = inp
        return h_prev * dec[..., None, None] + st, h_prev

    h0 = jnp.zeros((bsz, ng, hpg, hp, ns), jnp.float32)
    _, h_in = lax.scan(carry_state, h0, (jnp.moveaxis(states, 1, 0), jnp.moveaxis(chunk_decay, 3, 0)))
    h_in = jnp.moveaxis(h_in, 0, 1)
    y_off = jnp.einsum("bclgn,bcgepn,bgecl->bclgep", cc, h_in, jnp.exp(a_cum))
    return (y_diag + y_off).reshape(bsz, nc * SSM_CHUNK, nh, hp)[:, :seq]


def moba_attention(q, k, v):
    bsz, nh, seq, dh = q.shape
    q, k, v = (pad_seq(t, 2, MOBA_BLOCK) for t in (q, k, v))
    sp = q.shape[2]
    nb = sp // MOBA_BLOCK
    n_sel = min(MOBA_TOPK, nb)
    scale = dh ** -0.5
    kb = k.reshape(bsz, nh, nb, MOBA_BLOCK, dh)
    vb = v.reshape(bsz, nh, nb, MOBA_BLOCK, dh)
    k_mean = jnp.mean(kb.astype(jnp.float32), axis=3)
    q_blk = jnp.arange(sp) // MOBA_BLOCK
    gate = jnp.einsum("bhsd,bhnd->bhsn", q.astype(jnp.float32), k_mean)
    fully_past = jnp.arange(nb)[None, :] < q_blk[:, None]
    gate = jnp.where(fully_past, gate, -jnp.inf)
    _, sel = lax.top_k(gate, n_sel)
    n_steps = sp // MOBA_QCHUNK
    q_steps = q.reshape(bsz, nh, n_steps, MOBA_QCHUNK, dh).transpose(2, 0, 1, 3, 4)
    sel_steps = sel.reshape(bsz, nh, n_steps, MOBA_QCHUNK, n_sel).transpose(2, 0, 1, 3, 4)
    gather_blocks = jax.vmap(jax.vmap(lambda blocks, idx: blocks[idx]))

    def step(args):
        qc, sc, i = args
        t0 = i * MOBA_QCHUNK
        own = t0 // MOBA_BLOCK
        pos = t0 + jnp.arange(MOBA_QCHUNK)
        kpos = own * MOBA_BLOCK + jnp.arange(MOBA_BLOCK)
        k_own = lax.dynamic_index_in_dim(kb, own, axis=2, keepdims=False)
        v_own = lax.dynamic_index_in_dim(vb, own, axis=2, keepdims=False)
        s_own = jnp.einsum("bhqd,bhkd->bhqk", qc, k_own).astype(jnp.float32) * scale
        s_own = jnp.where(kpos[None, :] <= pos[:, None], s_own, -jnp.inf)
        k_sel = gather_blocks(kb, sc)
        v_sel = gather_blocks(vb, sc)
        s_sel = jnp.einsum("bhqd,bhqnkd->bhqnk", qc, k_sel).astype(jnp.float32) * scale
        valid = jnp.arange(n_sel) < own
        s_sel = jnp.where(valid[:, None], s_sel, -jnp.inf)
        logits = jnp.concatenate(
            [s_own, s_sel.reshape(bsz, nh, MOBA_QCHUNK, n_sel * MOBA_BLOCK)], axis=-1)
        p = jax.nn.softmax(logits, axis=-1).astype(v.dtype)
        p_own = p[..., :MOBA_BLOCK]
        p_sel = p[..., MOBA_BLOCK:].reshape(bsz, nh, MOBA_QCHUNK, n_sel, MOBA_BLOCK)
        return (jnp.einsum("bhqk,bhkd->bhqd", p_own, v_own)
                + jnp.einsum("bhqnk,bhqnkd->bhqd", p_sel, v_sel))

    out = lax.map(step, (q_steps, sel_steps, jnp.arange(n_steps)))
    return out.transpose(1, 2, 0, 3, 4).reshape(bsz, nh, sp, dh)[:, :, :seq]


def hybrid_token_mixer(h, w_in, q_norm_g, k_norm_g, conv_ssm_w, conv_ssm_b, dt_bias,
                       a_log, d_skip, ssm_norm_g, attn_norm_g, w_out):
    bsz, seq, _ = h.shape
    proj = h @ w_in
    o1 = D_ATTN
    o2 = 2 * D_ATTN
    o3 = 3 * D_ATTN
    o4 = o3 + D_SSM
    o5 = o4 + D_XBC
    q, k, v, z, xbc, dt = jnp.split(proj, [o1, o2, o3, o4, o5], axis=-1)
    q = rms_norm(q.reshape(bsz, seq, ATTN_HEADS, ATTN_HEAD_DIM), q_norm_g).transpose(0, 2, 1, 3)
    k = rms_norm(k.reshape(bsz, seq, ATTN_HEADS, ATTN_HEAD_DIM), k_norm_g).transpose(0, 2, 1, 3)
    v = v.reshape(bsz, seq, ATTN_HEADS, ATTN_HEAD_DIM).transpose(0, 2, 1, 3)
    y_attn = moba_attention(q, k, v).transpose(0, 2, 1, 3).reshape(bsz, seq, D_ATTN)
    y_attn = rms_norm(y_attn, attn_norm_g)
    xbc = jax.nn.silu(causal_dwconv(xbc, conv_ssm_w, conv_ssm_b))
    xs, bm, cm = jnp.split(xbc, [D_SSM, D_SSM + SSM_GROUPS * SSM_STATE], axis=-1)
    xs = xs.reshape(bsz, seq, SSM_HEADS, SSM_HEAD_DIM).astype(jnp.float32)
    dt = jax.nn.softplus((dt + dt_bias).astype(jnp.float32))
    a = -jnp.exp(a_log.astype(jnp.float32))
    y = ssd_chunked(xs * dt[..., None], dt * a,
                    bm.reshape(bsz, seq, SSM_GROUPS, SSM_STATE),
                    cm.reshape(bsz, seq, SSM_GROUPS, SSM_STATE))
    y = y + xs * d_skip.astype(jnp.float32)[:, None]
    y = y.reshape(bsz, seq, D_SSM) * jax.nn.silu(z.astype(jnp.float32))
    y_ssm = rms_norm(y.reshape(bsz, seq, SSM_GROUPS, D_SSM // SSM_GROUPS),
                     ssm_norm_g.reshape(SSM_GROUPS, D_SSM // SSM_GROUPS))
    y_ssm = y_ssm.reshape(bsz, seq, D_SSM).astype(h.dtype)
    return jnp.concatenate([y_attn, y_ssm], axis=-1) @ w_out


def conv_ffn(h, w_up, conv_w, conv_b, w_down):
    u = causal_dwconv(h @ w_up, conv_w, conv_b)
    g, val = jnp.split(u, 2, axis=-1)
    return (jax.nn.silu(g) * val) @ w_down


def setup_inputs(seed: int = 0) -> dict:
    key = jax.random.key(seed)
    ks = jax.random.split(key, 24)
    f32 = jnp.float32

    def nrm(k, shape, s):
        return jax.random.normal(k, shape, f32) * s

    def gain(k, shape):
        return 1.0 + 0.02 * jax.random.normal(k, shape, f32)

    dt0 = jnp.exp(jax.random.uniform(ks[10], (DEPTH, SSM_HEADS), f32)
                  * (math.log(DT_MAX) - math.log(DT_MIN)) + math.log(DT_MIN))
    return {
        "x": jax.random.normal(ks[0], (BATCH, SEQ, D_MODEL), f32),
        "c": jax.random.normal(ks[1], (BATCH, D_MODEL), f32),
        "w_ada": nrm(ks[2], (DEPTH, D_MODEL, N_MOD * D_MODEL), 0.5 * D_MODEL ** -0.5),
        "b_ada": nrm(ks[3], (DEPTH, N_MOD * D_MODEL), 0.02),
        "norm1_g": gain(ks[4], (DEPTH, D_MODEL)),
        "w_in": nrm(ks[5], (DEPTH, D_MODEL, D_IN), D_MODEL ** -0.5),
        "q_norm_g": gain(ks[6], (DEPTH, ATTN_HEAD_DIM)),
        "k_norm_g": gain(ks[7], (DEPTH, ATTN_HEAD_DIM)),
        "conv_ssm_w": nrm(ks[8], (DEPTH, SSM_CONV, D_XBC), SSM_CONV ** -0.5),
        "conv_ssm_b": nrm(ks[9], (DEPTH, D_XBC), 0.02),
        "dt_bias": dt0 + jnp.log(-jnp.expm1(-dt0)),
        "a_log": jnp.log(jax.random.uniform(ks[11], (DEPTH, SSM_HEADS), f32, minval=1.0, maxval=16.0)),
        "d_skip": gain(ks[12], (DEPTH, SSM_HEADS)),
        "ssm_norm_g": gain(ks[13], (DEPTH, D_SSM)),
        "attn_norm_g": gain(ks[14], (DEPTH, D_ATTN)),
        "w_out": nrm(ks[15], (DEPTH, D_MIX, D_MODEL), D_MIX ** -0.5),
        "norm2_g": gain(ks[16], (DEPTH, D_MODEL)),
        "w_up": nrm(ks[17], (DEPTH, D_MODEL, 2 * D_FF), D_MODEL ** -0.5),
        "conv_ffn_w": nrm(ks[18], (DEPTH, FFN_CONV, 2 * D_FF), FFN_CONV ** -0.5),
        "conv_ffn_b": nrm(ks[19], (DEPTH, 2 * D_FF), 0.02),
        "w_down": nrm(ks[20], (DEPTH, D_FF, D_MODEL), D_FF ** -0.5),
    }


def reference(x, c, w_ada, b_ada, norm1_g, w_in, q_norm_g, k_norm_g, conv_ssm_w, conv_ssm_b,
              dt_bias, a_log, d_skip, ssm_norm_g, attn_norm_g, w_out, norm2_g, w_up,
              conv_ffn_w, conv_ffn_b, w_down):
    for l in range(DEPTH):
        mod = jax.nn.silu(c) @ w_ada[l] + b_ada[l]
        shift1, scale1, gate1, shift2, scale2, gate2 = (
            m[:, None, :] for m in jnp.split(mod, N_MOD, axis=-1))
        h = rms_norm(x, norm1_g[l]) * (1 + scale1) + shift1
        x = x + gate1 * hybrid_token_mixer(
            h, w_in[l], q_norm_g[l], k_norm_g[l], conv_ssm_w[l], conv_ssm_b[l], dt_bias[l],
            a_log[l], d_skip[l], ssm_norm_g[l], attn_norm_g[l], w_out[l])
        h = rms_norm(x, norm2_g[l]) * (1 + scale2) + shift2
        x = x + gate2 * conv_ffn(h, w_up[l], conv_ffn_w[l], conv_ffn_b[l], w_down[l])
    return x
```

```python
import numpy as np
import concourse.bass as bass
import concourse.mybir as mybir
from concourse.alu_op_type import AluOpType as ALU
from concourse.bass_utils import run_bass_kernel_spmd

F32 = mybir.dt.float32
BF16 = mybir.dt.bfloat16
AF = mybir.ActivationFunctionType
AX = mybir.AxisListType
ENGS = ["pe", "act", "dve", "pool", "sp"]

D = 1024
S = 2048
NSEQ = 2
NT = S // 128
DFF = 2816
NJ = DFF // 128
EPS = 1e-6
NEG = -30000.0


class Prog:
    def __init__(self, nc):
        self.nc = nc
        self.ops = {e: [] for e in ENGS}
        self.last_w = {}
        self.readers = {}
        self.dma_keys = {}
        self.waited = {e: {} for e in ENGS}
        self.sems = {e: nc.alloc_semaphore("sem_" + e) for e in ENGS}

    def _deps(self, eng, reads, writes):
        deps = []

        def add(ev, raw):
            if ev is None:
                return
            w = self.waited[eng]
            if ev[0] == "c":
                if ev[1] == eng and eng == "pe":
                    return
                k = ("c", ev[1])
                if w.get(k, -1) >= ev[2]:
                    return
                w[k] = ev[2]
                deps.append(ev)
                self.ops[ev[1]][ev[2]]["signal"] = True
            else:
                k = ("d", ev[1])
                if w.get(k, -1) >= ev[2]:
                    return
                w[k] = ev[2]
                deps.append(ev)

        for t in reads:
            add(self.last_w.get(t), True)
        for t in writes:
            add(self.last_w.get(t), False)
            rd = self.readers.get(t)
            if rd:
                for ev in rd.values():
                    add(ev, False)
        return deps

    def _commit(self, ev, reads, writes):
        for t in reads:
            self.readers.setdefault(t, {})[(ev[0], ev[1])] = ev
        for t in writes:
            self.last_w[t] = ev
            self.readers[t] = {}

    def op(self, eng, name, reads, writes, *args, **kw):
        deps = self._deps(eng, reads, writes)
        idx = len(self.ops[eng])
        self.ops[eng].append({"name": name, "args": args, "kw": kw, "deps": deps,
                              "signal": False, "dma": None})
        self._commit(("c", eng, idx), reads, writes)

    def dma(self, eng, out, in_, key, reads=(), writes=()):
        deps = self._deps(eng, reads, writes)
        if key not in self.dma_keys:
            self.dma_keys[key] = [self.nc.alloc_semaphore("dsem%d" % len(self.dma_keys)), 0]
        ent = self.dma_keys[key]
        ent[1] += 16
        self.ops[eng].append({"name": "dma_start", "args": (), "kw": dict(out=out, in_=in_),
                              "deps": deps, "signal": False, "dma": (ent[0], ent[1])})
        self._commit(("d", key, ent[1]), reads, writes)

    def barrier(self):
        evs = []
        for e in ENGS:
            if self.ops[e]:
                idx = len(self.ops[e]) - 1
                while idx >= 0 and self.ops[e][idx]["name"] in (None, "dma_start"):
                    idx -= 1
                if idx >= 0:
                    evs.append(("c", e, idx))
        devs = [("d", k, v[1]) for k, v in self.dma_keys.items()]
        for e in ENGS:
            deps = []
            w = self.waited[e]
            for ev in evs:
                if ev[1] == e:
                    continue
                k = ("c", ev[1])
                if w.get(k, -1) >= ev[2]:
                    continue
                w[k] = ev[2]
                deps.append(ev)
                self.ops[ev[1]][ev[2]]["signal"] = True
            for ev in devs:
                k = ("d", ev[1])
                if w.get(k, -1) >= ev[2]:
                    continue
                w[k] = ev[2]
                deps.append(ev)
            self.ops[e].append({"name": None, "deps": deps, "signal": False, "dma": None})

    def wait_all_dma(self, eng="sp"):
        deps = [("d", k, v[1]) for k, v in self.dma_keys.items()]
        self.ops[eng].append({"name": None, "deps": deps, "signal": False, "dma": None})

    def emit(self):
        nc = self.nc
        ranks = {}
        for e in ENGS:
            r = 0
            rk = []
            for o in self.ops[e]:
                if o["signal"]:
                    r += 1
                rk.append(r)
            ranks[e] = rk

        def run(e, handle):
            regs = {}
            if e == "pool":
                regs = {"@zero": handle.to_reg(0.0), "@neg": handle.to_reg(NEG)}
            for o in self.ops[e]:
                if regs and o.get("name") == "affine_select":
                    o["kw"]["fill"] = regs[o["kw"]["fill"]]
                for ev in o["deps"]:
                    if ev[0] == "c":
                        handle.wait_ge(self.sems[ev[1]], ranks[ev[1]][ev[2]])
                    else:
                        handle.wait_ge(self.dma_keys[ev[1]][0], ev[2])
                if o["name"] is None:
                    continue
                inst = getattr(handle, o["name"])(*o["args"], **o["kw"])
                if o["dma"] is not None:
                    inst.then_inc(o["dma"][0], 16)
                elif o["signal"]:
                    inst.then_inc(self.sems[e], 1)

        with nc.Block() as block:
            @block.tensor
            def _(eng):
                run("pe", eng)

            @block.scalar
            def _(eng):
                run("act", eng)

            @block.vector
            def _(eng):
                run("dve", eng)

            @block.gpsimd
            def _(eng):
                run("pool", eng)

            @block.sync
            def _(eng):
                run("sp", eng)


_PF = {}
_o = 0
for _n, _w in [("b_ada", 48), ("g1", 8), ("g2", 8), ("gq", 1), ("gk", 1), ("cw", 48), ("cb", 12),
               ("dskip", 8), ("gssm", 8), ("gattn", 8), ("fw", 132), ("fb", 44)]:
    _PF[_n] = (_o, _w)
    _o += _w
NPF = _o


def build(nseq=NSEQ, dbg=None, phases=("attn", "ssd", "out", "ffn")):
    from contextlib import ExitStack, contextmanager
    nc = bass.Bass("TRN2", target_bir_lowering=False, dynamic_dma_scratch_size=4096)
    P = Prog(nc)
    dram = lambda n, s, k="ExternalInput": nc.dram_tensor(n, s, F32, kind=k).ap()
    x_d = dram("x", [NSEQ, S, D])
    cT_d = dram("cT", [128, 8, NSEQ])
    wada_d = dram("w_ada", [6, 128, 8, 1024])
    pfm_d = dram("pfm", [128, NPF])
    bgate_d = dram("bgate", [128, 1024])
    pbc_d = dram("pbc", [128, 32])
    wqkv_d = dram("w_qkv", [8, 128, 8, 3, 128])
    wzx_d = dram("w_zx", [8, 128, 8, 2, 128])
    wbc_d = dram("w_bc", [128, 8, 4, 128])
    wdt_d = dram("w_dt", [128, 8, 16])
    wout_d = dram("w_out", [128, 16, 1024])
    wup_d = dram("w_up", [NJ, 128, 8, 2, 128])
    wdn_d = dram("w_down", [8, 128, NJ, 128])
    y_d = dram("y", [NSEQ, S, D], "ExternalOutput")
    scr_d = nc.dram_tensor("scr_acum", [16, S], F32).ap()
    dbg_d = dram("dbg", [128, dbg], "ExternalOutput") if dbg else None

    cnt = [0]

    @contextmanager
    def Scope():
        with ExitStack() as es:
            yield es
        P.barrier()

    def sb(n, s, dt=F32, st=None, side=None):
        cnt[0] += 1
        nm = "%s_%d" % (n, cnt[0])
        if st is None:
            return nc.alloc_sbuf_tensor(nm, s, dt)
        if side is not None:
            return st.enter_context(nc.sbuf_tensor(nm, s, dt, side=side))
        return st.enter_context(nc.sbuf_tensor(nm, s, dt))

    ps = [nc.alloc_psum_tensor("ps%d" % i, [128, 512], F32) for i in range(8)]
    psb = [p[:].bitcast(BF16) for p in ps]
    pst = ["ps%d" % i for i in range(8)]
    rot = {"A": 0}

    def bankA():
        i = rot["A"] % 4
        rot["A"] += 1
        return i

    def MM(out, lhsT, rhs, start, stop, r, w):
        P.op("pe", "matmul", r, w, out, lhsT=lhsT, rhs=rhs, start=start, stop=stop)

    def TR(out, in_, ident, r, w):
        P.op("pe", "transpose", r, w, out=out, in_=in_, identity=ident)

    def ACT(out, in_, func, r, w, **kw):
        P.op("act", "activation", r, w, out=out, in_=in_, func=func, **kw)

    def TT(eng, out, in0, in1, op, r, w):
        P.op(eng, "tensor_tensor", r, w, out=out, in0=in0, in1=in1, op=op)

    def TS(eng, out, in0, s1, s2, op0, op1, r, w):
        if op1 is None and eng == "pool":
            s2, op1 = 0.0, ALU.add
        if op1 is None:
            P.op(eng, "tensor_scalar", r, w, out=out, in0=in0, scalar1=s1, scalar2=None, op0=op0)
        else:
            P.op(eng, "tensor_scalar", r, w, out=out, in0=in0, scalar1=s1, scalar2=s2, op0=op0, op1=op1)

    def STT(out, in0, scalar, in1, op0, op1, r, w):
        P.op("dve", "scalar_tensor_tensor", r, w, out=out, in0=in0, scalar=scalar, in1=in1, op0=op0, op1=op1)

    def CP(eng, out, in_, r, w):
        if eng == "act":
            ACT(out, in_, AF.Copy, r, w)
        else:
            P.op(eng, "tensor_copy", r, w, out=out, in_=in_)

    def dump(ap, col, toks):
        if dbg_d is not None:
            n = 1
            for d_ in ap.shape[1:]:
                n *= d_
            P.dma("pool", dbg_d[0:ap.shape[0], col:col + n], ap, ("dbg", col), reads=toks)

    ident_f = sb("ident_f", [128, 128])
    ident_b = sb("ident_b", [128, 128], BF16)
    ones_f = sb("ones_f", [128, 128])
    ones_b = sb("ones_b", [128, 128], BF16)
    eblk = sb("eblk", [128, 8, 128], BF16)
    T0 = sb("T0", [128, 256])
    T1 = sb("T1", [128, 256])
    epsT = sb("epsT", [128, 1])
    oneT = sb("oneT", [128, 1])
    pfm = sb("pfm_sb", [128, NPF])
    pbc = sb("pbc_sb", [128, 32])
    a_bc = sb("a_bc", [128, 16])
    modT = sb("modT", [128, 6, 8, NSEQ])
    s1 = sb("s1", [128, 8, NSEQ])
    s2 = sb("s2", [128, 8, NSEQ])
    gate1_bc = sb("gate1_bc", [128, NSEQ, 1024])
    cT = sb("cT_sb", [128, 8, NSEQ])
    sc_b = sb("sc_b", [128, 8, NSEQ], BF16)
    gm = sb("gm", [128, 8, 8])
    top8 = sb("top8", [128, 8, 8])
    negm = sb("negm", [128, 8, 8])
    nmT = sb("nmT", [128, S], BF16)
    ssq_mix = sb("ssq_mix", [128, 3, 16])
    rstd_mix = sb("rstd_mix", [128, 3, 16])
    stat = sb("stat", [128, 3, 8])
    hT = sb("hT", [128, 8, S], BF16)

    def pf(name, i=0, n=1):
        o, w = _PF[name]
        return pfm[:, o + i:o + i + n]

    P.dma("sp", pfm[:], pfm_d, "pfm", writes=["pfm"])
    P.dma("sp", pbc[:], pbc_d, "pbc", writes=["pbc"])
    P.dma("sp", cT[:], cT_d, "cT", writes=["cT"])
    for b in range(NSEQ):
        P.dma("sp", gate1_bc[:, b, :], bgate_d, ("g1bc", b), writes=["gate1_bc"])
    with Scope() as st:
        big_ones = sb("big_ones", [128, 1024], BF16, st)
        ones256 = sb("ones256", [128, 256], F32, st)
        sc_rep = sb("sc_rep", [128, 8, NSEQ, 128], BF16, st)
        wa = [sb("wa", [128, 8, 1024], BF16, st) for _ in range(2)]
        P.op("pool", "memset", [], ["ones_f"], ones_f[:], 1.0)
        P.op("pool", "memset", [], ["ones_b"], ones_b[:], 1.0)
        P.op("pool", "memset", [], ["big_ones"], big_ones[:], 1.0)
        P.op("pool", "memset", [], ["ones256"], ones256[:], 1.0)
        P.op("pool", "memset", [], ["epsT"], epsT[:], EPS)
        P.op("pool", "memset", [], ["oneT"], oneT[:], 1.0)
        P.op("pool", "memset", [], ["nmT0"], nmT[:], 0.0)
        P.op("pool", "memset", [], ["gm"], gm[:], -1e30)
        P.op("pool", "affine_select", ["ones_f"], ["ident_f"], out=ident_f[:], in_=ones_f[:], pattern=[[-1, 128]],
             compare_op=ALU.is_equal, fill="@zero", base=0, channel_multiplier=1)
        P.op("pool", "affine_select", ["ones_b"], ["ident_b"], out=ident_b[:], in_=ones_b[:], pattern=[[-1, 128]],
             compare_op=ALU.is_equal, fill="@zero", base=0, channel_multiplier=1)
        P.op("pool", "affine_select", ["big_ones"], ["eblk"], out=eblk[:].rearrange("p a b -> p (a b)"),
             in_=big_ones[:], pattern=[[-1, 8], [0, 128]], compare_op=ALU.is_equal, fill="@zero", base=0,
             channel_multiplier=1)
        P.op("pool", "affine_select", ["ones256"], ["T0"], out=T0[:], in_=ones256[:], pattern=[[1, 256]],
             compare_op=ALU.is_ge, fill="@zero", base=0, channel_multiplier=-1)
        P.op("pool", "affine_select", ["ones256"], ["T1"], out=T1[:], in_=ones256[:], pattern=[[1, 256]],
             compare_op=ALU.is_ge, fill="@zero", base=-128, channel_multiplier=-1)
        ACT(a_bc[:], pbc[:, 16:32], AF.Exp, ["pbc"], ["a_bc"])
        TS("dve", a_bc[:], a_bc[:], -1.0, None, ALU.mult, None, ["a_bc"], ["a_bc"])
        ACT(cT[:], cT[:], AF.Silu, ["cT"], ["cT"])
        CP("dve", sc_b[:], cT[:], ["cT"], ["sc_b"])
        CP("dve", sc_rep[:], cT[:].unsqueeze(3).to_broadcast([128, 8, NSEQ, 128]), ["cT"], ["sc_rep"])
        ob, _ = _PF["b_ada"]
        for m in range(6):
            wb = wa[m % 2]
            wt = "wa%d" % (m % 2)
            P.dma("pool", wb[:], wada_d[m], wt, writes=[wt])
            if m == 2:
                for b in range(NSEQ):
                    for nch in range(2):
                        bi = bankA()
                        for k in range(8):
                            MM(ps[bi][:], sc_rep[:, k, b, :], wb[:, k, nch * 512:(nch + 1) * 512], k == 0, k == 7,
                               [wt, "sc_rep"], [pst[bi]])
                        gsl = gate1_bc[:, b, nch * 512:(nch + 1) * 512]
                        TT("dve", gsl, ps[bi][:], gsl, ALU.add, [pst[bi], "gate1_bc"], ["gate1_bc"])
            else:
                bi = bankA()
                for fc in range(8):
                    for k in range(8):
                        MM(ps[bi][:, fc * NSEQ:(fc + 1) * NSEQ], wb[:, k, fc * 128:(fc + 1) * 128], sc_b[:, k, :],
                           k == 0, k == 7, [wt, "sc_b"], [pst[bi]])
                TT("dve", modT[:, m, :, :], ps[bi][:, 0:8 * NSEQ].rearrange("p (a b) -> p a b", b=NSEQ),
                   pfm[:, ob + m * 8:ob + m * 8 + 8].unsqueeze(2).to_broadcast([128, 8, NSEQ]), ALU.add,
                   [pst[bi], "pfm"], ["modT"])
        for (sx, mi, gname) in ((s1, 1, "g1"), (s2, 4, "g2")):
            og, _ = _PF[gname]
            TS("dve", sx[:], modT[:, mi, :, :], 1.0, None, ALU.add, None, ["modT"], ["sx"])
            TT("dve", sx[:], sx[:], pfm[:, og:og + 8].unsqueeze(2).to_broadcast([128, 8, NSEQ]), ALU.mult,
               ["sx", "pfm"], ["sx"])

    nrm = {"i": 0}

    hTt = ["hT"] * 8

    def norm_to_hT(st, b, src_fn, scl, shift_m):
        xn = [sb("xn", [128, 1024], F32, st) for _ in range(4)]
        junk = [sb("junk", [128, 1024], F32, st) for _ in range(2)]
        for q4 in range(4):
            srcs = [src_fn(q4 * 4 + i) for i in range(4)]
            sl0 = (nrm["i"] % 2) * 4
            nrm["i"] += 1
            stt = ("stat", sl0)
            for i in range(4):
                ACT(junk[i % 2][:], srcs[i][0], AF.Square, [srcs[i][1]], ["junk%d" % (i % 2)])
                P.op("dve", "tensor_reduce", ["junk%d" % (i % 2)], [stt], out=stat[:, 0, sl0 + i:sl0 + i + 1], in_=junk[i % 2][:],
                     axis=AX.X, op=ALU.add)
            for i in range(4):
                ACT(stat[:, 1, sl0 + i:sl0 + i + 1], stat[:, 0, sl0 + i:sl0 + i + 1], AF.Sqrt, [stt, "epsT"], [stt], scale=1.0 / D,
                    bias=epsT[:, 0:1])
            for i in range(4):
                P.op("dve", "reciprocal", [stt], [stt], out=stat[:, 2, sl0 + i:sl0 + i + 1], in_=stat[:, 1, sl0 + i:sl0 + i + 1])
            for i in range(4):
                eng = "pool"
                TS(eng, xn[i][:], srcs[i][0], stat[:, 2, sl0 + i:sl0 + i + 1], None, ALU.mult, None, [srcs[i][1], stt], ["xn%d" % i])
            for h2 in range(2):
                hg = q4 * 2 + h2
                base = 0 if hg % 2 == 0 else 4
                for i in range(2):
                    xb_, xnt = xn[h2 * 2 + i], "xn%d" % (h2 * 2 + i)
                    for k in range(8):
                        bi = base + k // 2
                        c0 = ((k % 2) * 2 + i) * 128
                        TR(ps[bi][:, c0:c0 + 128], xb_[:, k * 128:(k + 1) * 128], ident_f[:], [xnt, "ident_f"], [pst[bi]])
                for k in range(8):
                    bi = base + k // 2
                    c0 = (k % 2) * 256
                    dst = hT[:, k, hg * 256:(hg + 1) * 256]
                    if k % 2 == 0:
                        ACT(dst, ps[bi][:, c0:c0 + 256], AF.Identity, [pst[bi], "sx", "modT"], [hTt[k]],
                            scale=scl[:, k, b:b + 1], bias=modT[:, shift_m, k, b:b + 1])
                    else:
                        TS("dve", dst, ps[bi][:, c0:c0 + 256], scl[:, k, b:b + 1], modT[:, shift_m, k, b:b + 1],
                           ALU.mult, ALU.add, [pst[bi], "sx", "modT"], [hTt[k]])

    def proj(w_ap_fn, wtok, g, extra_r=()):
        bi = bankA()
        for k in range(8):
            MM(ps[bi][:], w_ap_fn(k), hT[:, k, g * 512:(g + 1) * 512], k == 0, k == 7,
               [wtok, hTt[k]] + list(extra_r), [pst[bi]])
        return bi

    for b in range(nseq):
        st_seq = ExitStack()
        if True:
            ymixT = sb("ymixT", [128, 16, S], BF16, st_seq)
            with Scope() as st:
                xts = [sb("xt", [128, 1024], F32, st) for _ in range(4)]

                def src1(tt, xts=xts, b=b):
                    t = xts[tt % 4]
                    tok = "xt%d" % (tt % 4)
                    P.dma("sp", t[:], x_d[b, tt * 128:(tt + 1) * 128, :], tok, writes=[tok])
                    return t[:], tok
                norm_to_hT(st, b, src1, s1, 0)
            P.op("pool", "memset", [], ["ssq_mix"], ssq_mix[:], 0.0)
            if dbg and b == 0:
                dump(hT[:, 0, 0:512], 0, [hTt[0]])

            if "attn" in phases:
              with Scope() as st:
                wq = [sb("wqkv", [128, 8, 3, 128], BF16, st) for _ in range(2)]
                qf = [sb("qf", [128, 512], F32, st) for _ in range(2)]
                sq = [sb("sq", [128, 512], BF16, st) for _ in range(2)]
                lnv = sb("lnv", [128, 512], F32, st)
                rs = sb("rs", [128, 512], F32, st)
                qT = [sb("qT", [128, S], BF16, st) for _ in range(2)]
                kT = [sb("kT", [128, S], BF16, st) for _ in range(2)]
                vT = sb("vT", [128, S], BF16, st)
                vtok = [sb("vtok", [128, 16, 128], BF16, st) for _ in range(2)]
                nm2 = [nmT, sb("nmT2", [128, S], BF16, st)]
                PT = [sb("PT", [128, 512], BF16, st) for _ in range(4)]
                lnd = sb("lnd", [128, 512], F32, st)
                rd = sb("rd", [128, 512], F32, st)
                yf = sb("yf", [128, 512], F32, st)
                ysq = sb("ysq", [128, 512], BF16, st)
                kmf = sb("kmf", [128, 8], F32, st)
                kmb = sb("kmb", [128, 8], BF16, st)
                P.op("pool", "memset", [], ["nmT1"], nm2[1][:], 0.0)
                ptc = [0]
                uct = [0]

                def wq_issue(h):
                    if h < 8:
                        P.dma("pool", wq[h % 2][:], wqkv_d[h], "wqkv%d" % (h % 2), writes=["wqkv%d" % (h % 2)])

                def proj_units(h):
                    hb = h % 2
                    wtk = "wqkv%d" % hb
                    qTt, kTt, vtt, nmt = "qT%d" % hb, "kT%d" % hb, "vtok%d" % hb, "nmT%d" % hb
                    units = []
                    for ti, (dstT, dtok, gn) in enumerate(((qT[hb], qTt, "gq"), (kT[hb], kTt, "gk"))):
                        for g in range(4):
                            sl = (ti * 4 + g) % 2

                            def u1(ti=ti, g=g, sl=sl):
                                bi = proj(lambda k: wq[hb][:, k, ti, :], wtk, g)
                                CP("dve", qf[sl][:], ps[bi][:], [pst[bi]], ["qf%d" % sl])
                                TT("pool", sq[sl][:], qf[sl][:], qf[sl][:], ALU.mult, ["qf%d" % sl], ["sq%d" % sl])

                            def u2(g=g, sl=sl, dstT=dstT, dtok=dtok, gn=gn):
                                b2 = bankA()
                                MM(ps[b2][:], ones_b[:], sq[sl][:], True, True, ["ones_b", "sq%d" % sl], [pst[b2]])
                                ACT(lnv[:], ps[b2][:], AF.Ln, [pst[b2], "epsT"], ["lnv"], scale=1.0 / 128, bias=epsT[:, 0:1])
                                ACT(rs[:], lnv[:], AF.Exp, ["lnv"], ["rs"], scale=-0.5)
                                STT(dstT[:, g * 512:(g + 1) * 512], qf[sl][:], pf(gn), rs[:], ALU.mult, ALU.mult,
                                    ["qf%d" % sl, "rs", "pfm"], [dtok])
                            units += [u1, u2]
                    u1s, u2s = units[0::2], units[1::2]
                    units = [u1s[0]]
                    for k_ in range(1, 8):
                        units += [u1s[k_], u2s[k_ - 1]]
                    units.append(u2s[7])
                    for g in range(4):
                        def uv1(g=g):
                            bi = proj(lambda k: wq[hb][:, k, 2, :], wtk, g)
                            CP("act", vT[:, g * 512:(g + 1) * 512], ps[bi][:], [pst[bi]], ["vT"])

                        def uv2(g=g):
                            bi = bankA()
                            for i in range(4):
                                tt = g * 4 + i
                                TR(psb[bi][:, i * 128:(i + 1) * 128], vT[:, tt * 128:(tt + 1) * 128], ident_b[:],
                                   ["vT", "ident_b"], [pst[bi]])
                            CP("dve", vtok[hb][:, g * 4:(g + 1) * 4, :].rearrange("p a b -> p (a b)"), psb[bi][:, 0:512],
                               [pst[bi]], [vtt])
                        units += [uv1, uv2]

                    def ug1():
                        P.op("dve", "tensor_reduce", [kTt], ["kmf"], out=kmf[:], in_=kT[hb][:].rearrange("p (n t) -> p n t", t=256),
                             axis=AX.X, op=ALU.add)
                        CP("dve", kmb[:], kmf[:], ["kmf"], ["kmb"])

                    gst = {}

                    def ug2():
                        bi = bankA()
                        gst["bi"] = bi
                        for i in range(8):
                            tt = 8 + i
                            MM(ps[bi][:, i * 8:(i + 1) * 8], qT[hb][:, tt * 128:(tt + 1) * 128], kmb[:], True, True,
                               [qTt, "kmb"], [pst[bi]])
                        for i in range(8):
                            own = (8 + i) // 2
                            CP("dve", gm[:, i, 0:own], ps[bi][:, i * 8:i * 8 + own], [pst[bi], "gm"], ["gm"])
                        for i in range(8):
                            P.op("dve", "max", ["gm"], ["top8"], out=top8[:, i, :], in_=gm[:, i, :])
                        TT("dve", negm[:], gm[:], top8[:, :, 2:3].to_broadcast([128, 8, 8]), ALU.is_lt, ["gm", "top8"], ["negm"])
                        TS("dve", negm[:], negm[:], NEG, None, ALU.mult, None, ["negm"], ["negm"])

                    def ug3():
                        for g2 in range(2):
                            bi = bankA()
                            for i in range(4):
                                TR(ps[bi][0:8, i * 128:(i + 1) * 128], negm[:, g2 * 4 + i, :], ident_f[:], ["negm", "ident_f"],
                                   [pst[bi]])
                            CP("dve", nm2[hb][0:8, 1024 + g2 * 512:1024 + (g2 + 1) * 512], ps[bi][0:8, :], [pst[bi]], [nmt])
                    units += [ug1, ug2, ug3]
                    return units

                def main_head(h, side):
                    hb = h % 2
                    qTt, kTt, vtt, nmt = "qT%d" % hb, "kT%d" % hb, "vtok%d" % hb, "nmT%d" % hb

                    def stage1(j, kt):
                        blk = kt // 2
                        A, Bk = 2 * j, 2 * j + 1
                        c0 = 256 if blk == Bk else 0
                        bi = bankA()
                        need_mask = (j >= 2) and (blk < Bk)
                        MM(ps[bi][:, c0:512], kT[hb][:, kt * 128:(kt + 1) * 128], qT[hb][:, j * 512 + c0:(j + 1) * 512],
                           True, not need_mask, [kTt, qTt], [pst[bi]])
                        if need_mask:
                            m0 = 256 if blk == A else 0
                            MM(ps[bi][:, m0:512], eblk[:, blk, :], nm2[hb][:, j * 512 + m0:(j + 1) * 512], False, True,
                               ["eblk", nmt], [pst[bi]])
                        pt = PT[ptc[0] % 4]
                        ptk = "PT%d" % (ptc[0] % 4)
                        ptc[0] += 1
                        ACT(pt[:, c0:512], ps[bi][:, c0:512], AF.Exp, [pst[bi]], [ptk], scale=128 ** -0.5)
                        if blk >= A:
                            r = kt % 2
                            P.op("pool", "affine_select", [ptk], [ptk], out=pt[:, c0:c0 + 256], in_=pt[:, c0:c0 + 256],
                                 pattern=[[1, 256]], compare_op=ALU.is_ge, fill="@zero", base=-128 * r,
                                 channel_multiplier=-1)
                        return (j, kt, c0, pt, ptk)

                    def stage2(ctx):
                        j, kt, c0, pt, ptk = ctx
                        bo, bd = 4 + (j % 2) * 2, 5 + (j % 2) * 2
                        nk = 4 * (j + 1)
                        MM(ps[bo][:, c0:512], vtok[hb][:, kt, :], pt[:, c0:512], kt == 0, kt == nk - 1,
                           [vtt, ptk], [pst[bo]])
                        MM(ps[bd][:, c0:512], ones_b[:], pt[:, c0:512], kt == 0, kt == nk - 1,
                           ["ones_b", ptk], [pst[bd]])
                        if kt == nk - 1:
                            ACT(lnd[:], ps[bd][:], AF.Ln, [pst[bd]], ["lnd"])
                            ACT(rd[:], lnd[:], AF.Exp, ["lnd"], ["rd"], scale=-1.0)
                            TT("dve", yf[:], ps[bo][:], rd[:], ALU.mult, [pst[bo], "rd"], ["yf"])
                            TT("pool", ysq[:], yf[:], yf[:], ALU.mult, ["yf"], ["ysq"])
                            TS("pool", ymixT[:, h, j * 512:(j + 1) * 512], yf[:], pf("gattn", h), None, ALU.mult, None,
                               ["yf", "pfm"], ["ymixT"])
                            def ssq_fn(j=j):
                                bi = bankA()
                                for i in range(4):
                                    MM(ps[bi][:, i:i + 1], ysq[:, i * 128:(i + 1) * 128], ones_b[:, 0:1], True, True,
                                       ["ysq", "ones_b"], [pst[bi]])
                                TT("dve", ssq_mix[:, 0, j * 4:(j + 1) * 4], ssq_mix[:, 0, j * 4:(j + 1) * 4], ps[bi][:, 0:4],
                                   ALU.add, [pst[bi], "ssq_mix"], ["ssq_mix"])
                            defer.append([4, ssq_fn])

                    tiles = [(j, kt) for j in range(4) for kt in range(4 * (j + 1))]
                    pend = []
                    defer = []
                    for (j, kt) in tiles:
                        pend.append(stage1(j, kt))
                        if len(pend) > 3:
                            stage2(pend.pop(0))
                        for d_ in list(defer):
                            d_[0] -= 1
                            if d_[0] <= 0:
                                defer.remove(d_)
                                d_[1]()
                        if side:
                            side.pop(0)()
                    while pend:
                        stage2(pend.pop(0))
                    while side:
                        side.pop(0)()
                    for d_ in defer:
                        d_[1]()

                wq_issue(0)
                wq_issue(1)
                for u in proj_units(0):
                    u()
                for h in range(8):
                    side = proj_units(h + 1) if h + 1 < 8 else []
                    main_head(h, side)
                    wq_issue(h + 2)
            if dbg and b == 0:
                dump(ymixT[:, 0, 0:512], 512, ["ymixT"])
                dump(ymixT[:, 7, 1536:2048], 1024, ["ymixT"])

            if "ssd" in phases:
              with Scope() as st:
                wdt = sb("wdt", [128, 8, 16], BF16, st)
                wbc = sb("wbc", [128, 8, 4, 128], BF16, st)
                wzx = [sb("wzx", [128, 8, 2, 128], BF16, st) for _ in range(3)]
                dtr = sb("dtr", [128, 16, 16], F32, st)
                dtl = sb("dtl", [128, 16, 16], F32, st)
                dt_tok = sb("dt_tok", [128, 16, 16], F32, st)
                da_tok = sb("da_tok", [128, 16, 16], F32, st)
                nacum = sb("nacum", [128, 16, 16], F32, st)
                tot = sb("tot", [128, 8, 16], F32, st)
                cdec = sb("cdec", [128, 8, 16], F32, st)
                dtx = sb("dtx", [128, 16, 16], F32, st)
                acs = sb("acs", [16, 512], F32, st)
                raw = [sb("raw", [128, 515], F32, st) for _ in range(2)]
                halo = sb("halo", [128, 12, 3], F32, st)
                acc = [sb("acc", [128, 512], F32, st) for _ in range(3)]
                szT = [sb("szT", [128, 512], F32, st) for _ in range(3)]
                BT = sb("BT", [128, 2, 512], BF16, st)
                CTt = sb("CT", [128, 2, 512], BF16, st)
                Btok = sb("Btok", [128, 2, 4, 128], BF16, st)
                cb = sb("cb", [128, 2, 2, 384], BF16, st)
                xdt = [sb("xdt", [128, 4, 128], BF16, st) for _ in range(2)]
                xdtt = [sb("xdtt", [128, 4, 128], BF16, st) for _ in range(2)]
                bcs = [sb("bc", [128, 512], F32, st) for _ in range(6)]
                tmp = [sb("tmp", [128, 384], F32, st) for _ in range(4)]
                dec = [sb("dec", [128, 384], BF16, st) for _ in range(4)]
                MT = [sb("MT", [128, 384], BF16, st) for _ in range(4)]
                ebc = [sb("ebc", [128, 256], BF16, st) for _ in range(4)]
                Ct = [sb("Ct", [128, 256], BF16, st) for _ in range(4)]
                hin = sb("hin", [128, 16, 64], F32, st)
                hinb = [sb("hinb", [128, 2, 2, 64], BF16, st) for _ in range(2)]
                yv = sb("yv", [128, 512], F32, st)
                yg = sb("yg", [128, 512], F32, st)
                ysq2 = sb("ysq2", [128, 512], BF16, st)
                P.dma("pool", wdt[:], wdt_d, "wdt", writes=["wdt"])
                P.dma("pool", wbc[:], wbc_d, "wbc", writes=["wbc"])
                P.op("pool", "memset", [], ["hin"], hin[:], 0.0)
                P.op("pool", "memset", [], ["halo"], halo[:], 0.0)
                bi = bankA()
                for tt in range(16):
                    for k in range(8):
                        MM(ps[bi][:, tt * 16:(tt + 1) * 16], hT[:, k, tt * 128:(tt + 1) * 128], wdt[:, k, :], k == 0, k == 7,
                           [hTt[k], "wdt"], [pst[bi]])
                TT("dve", dtr[:], ps[bi][:, 0:256].rearrange("p (a b) -> p a b", b=16),
                   pbc[:, 0:16].unsqueeze(1).to_broadcast([128, 16, 16]), ALU.add, [pst[bi], "pbc"], ["dtr"])
                STT(dtl[:], dtr[:], -1.0, dtr[:], ALU.mult, ALU.max, ["dtr"], ["dtl"])
                ACT(dtl[:], dtl[:], AF.Exp, ["dtl"], ["dtl"], scale=-1.0)
                ACT(dtl[:], dtl[:], AF.Ln, ["dtl", "oneT"], ["dtl"], bias=oneT[:, 0:1])
                STT(dt_tok[:], dtr[:], 0.0, dtl[:], ALU.max, ALU.add, ["dtr", "dtl"], ["dt_tok"])
                TT("dve", da_tok[:], dt_tok[:], a_bc[:].unsqueeze(1).to_broadcast([128, 16, 16]), ALU.mult,
                   ["dt_tok", "a_bc"], ["da_tok"])
                b1, b2, b3 = bankA(), bankA(), bankA()
                for c in range(8):
                    t0_, t1_ = 2 * c, 2 * c + 1
                    MM(ps[b1][:, t0_ * 16:(t0_ + 1) * 16], T0[:, 0:128], da_tok[:, t0_, :], True, True, ["T0", "da_tok"], [pst[b1]])
                    MM(ps[b1][:, t1_ * 16:(t1_ + 1) * 16], T0[:, 128:256], da_tok[:, t0_, :], True, False, ["T0", "da_tok"], [pst[b1]])
                    MM(ps[b1][:, t1_ * 16:(t1_ + 1) * 16], T1[:, 128:256], da_tok[:, t1_, :], False, True, ["T1", "da_tok"], [pst[b1]])
                    MM(ps[b2][:, c * 16:(c + 1) * 16], ones_f[:], da_tok[:, t0_, :], True, False, ["ones_f", "da_tok"], [pst[b2]])
                    MM(ps[b2][:, c * 16:(c + 1) * 16], ones_f[:], da_tok[:, t1_, :], False, True, ["ones_f", "da_tok"], [pst[b2]])
                TS("dve", nacum[:].rearrange("p a b -> p (a b)"), ps[b1][:, 0:256], -1.0, None, ALU.mult, None, [pst[b1]], ["nacum"])
                CP("dve", tot[:].rearrange("p a b -> p (a b)"), ps[b2][:, 0:128], [pst[b2]], ["tot"])
                ACT(cdec[:], tot[:], AF.Exp, ["tot"], ["cdec"])
                TT("dve", dtx[:].rearrange("p (c t) h -> p c t h", t=2), nacum[:].rearrange("p (c t) h -> p c t h", t=2),
                   tot[:].unsqueeze(2).to_broadcast([128, 8, 2, 16]), ALU.add, ["nacum", "tot"], ["dtx"])
                ACT(dtx[:], dtx[:], AF.Exp, ["dtx"], ["dtx"])
                TT("dve", dtx[:], dtx[:], dt_tok[:], ALU.mult, ["dtx", "dt_tok"], ["dtx"])
                for g in range(4):
                    for cc in range(2):
                        c = g * 2 + cc
                        MM(ps[b3][0:16, cc * 256:(cc + 1) * 256], da_tok[:, 2 * c, :], T0[:], True, False, ["T0", "da_tok"], [pst[b3]])
                        MM(ps[b3][0:16, cc * 256:(cc + 1) * 256], da_tok[:, 2 * c + 1, :], T1[:], False, True, ["T1", "da_tok"], [pst[b3]])
                    CP("dve", acs[:], ps[b3][0:16, :], [pst[b3]], ["acs"])
                    P.dma("sp", scr_d[:, g * 512:(g + 1) * 512], acs[:], "acs", reads=["acs"], writes=[("scr", g)])

                def conv_silu(bi, cc, rw, rwt, out_ap, wtoks, g, a_, at):
                    oc, _ = _PF["cw"]
                    obb, _ = _PF["cb"]
                    CP("pool", rw[:, 0:3], halo[:, cc, :], ["halo"], [rwt])
                    CP("act", rw[:, 3:515], ps[bi][:], [pst[bi]], [rwt])
                    CP("pool", halo[:, cc, :], rw[:, 512:515], [rwt], ["halo"])
                    ACT(a_[:], rw[:, 0:512], AF.Identity, [rwt, "pfm"], [at], scale=pfm[:, oc + cc * 4:oc + cc * 4 + 1],
                        bias=pfm[:, obb + cc:obb + cc + 1])
                    for j_ in range(1, 4):
                        STT(a_[:], rw[:, j_:j_ + 512], pfm[:, oc + cc * 4 + j_:oc + cc * 4 + j_ + 1], a_[:], ALU.mult, ALU.add,
                            [rwt, at, "pfm"], [at])
                    ACT(out_ap, a_[:], AF.Silu, [at], wtoks)

                def wzx_issue(idx):
                    if idx < 32:
                        wt_ = "wzx%d" % (idx % 3)
                        P.dma("pool", wzx[idx % 3][:], wzx_d[idx % 8], wt_, writes=[wt_])

                def BCgroup(g):
                    for idx in range(4):
                        cc = 8 + idx
                        bi = proj(lambda k, idx=idx: wbc[:, k, idx, :], "wbc", g)
                        dst = BT[:, idx, :] if idx < 2 else CTt[:, idx - 2, :]
                        conv_silu(bi, cc, raw[idx % 2], "raw%d" % (idx % 2), dst, ["BT"] if idx < 2 else ["CT"], g,
                                  (yv, yg)[idx % 2], ("yv", "yg")[idx % 2])
                    for gi in range(2):
                        bi = bankA()
                        for i in range(4):
                            TR(psb[bi][:, i * 128:(i + 1) * 128], BT[:, gi, i * 128:(i + 1) * 128], ident_b[:], ["BT", "ident_b"], [pst[bi]])
                        CP("dve", Btok[:, gi, :, :].rearrange("p a b -> p (a b)"), psb[bi][:, 0:512], [pst[bi]], ["Btok"])
                        for cc in range(2):
                            bi = bankA()
                            MM(ps[bi][:, 0:256], BT[:, gi, cc * 256:cc * 256 + 128], CTt[:, gi, cc * 256:(cc + 1) * 256], True, True,
                               ["BT", "CT"], [pst[bi]])
                            MM(ps[bi][:, 256:384], BT[:, gi, cc * 256 + 128:cc * 256 + 256], CTt[:, gi, cc * 256 + 128:(cc + 1) * 256],
                               True, True, ["BT", "CT"], [pst[bi]])
                            CP("act", cb[:, gi, cc, :], ps[bi][:, 0:384], [pst[bi]], ["cb"])

                def stA1(g, hp):
                    idx = g * 8 + hp
                    wb_ = wzx[idx % 3]
                    wt_ = "wzx%d" % (idx % 3)
                    wzx_issue(idx + 2)
                    p3 = hp % 3
                    for hl in range(2):
                        hh = 2 * hp + hl
                        bk = (2 * hp + hl) % 6
                        P.dma("sp", bcs[bk][:], scr_d[hh:hh + 1, g * 512:(g + 1) * 512].partition_broadcast(128), "bc%d" % bk,
                              reads=[("scr", g)], writes=["bc%d" % bk])
                    bi = proj(lambda k: wb_[:, k, 0, :], wt_, g)
                    ACT(szT[p3][:], ps[bi][:], AF.Silu, [pst[bi]], ["szT%d" % p3])
                    bi = proj(lambda k: wb_[:, k, 1, :], wt_, g)
                    xs, xst = acc[p3], "acc%d" % p3
                    conv_silu(bi, hp, raw[hp % 2], "raw%d" % (hp % 2), xs[:], [xst], g, xs, xst)

                def stA2(g, hp):
                    pp = hp % 2
                    xs, xst = acc[hp % 3], "acc%d" % (hp % 3)
                    bi = bankA()
                    for i in range(4):
                        TR(ps[bi][:, i * 128:(i + 1) * 128], xs[:, i * 128:(i + 1) * 128], ident_f[:], [xst, "ident_f"], [pst[bi]])
                    pv = ps[bi][:].rearrange("p (a h d) -> p a h d", a=4, h=2)
                    TT("dve", xdt[pp][:].rearrange("p a (h d) -> p a h d", h=2), pv,
                       dt_tok[:, g * 4:(g + 1) * 4, 2 * hp:2 * hp + 2].unsqueeze(3).to_broadcast([128, 4, 2, 64]), ALU.mult,
                       [pst[bi], "dt_tok"], ["xdt%d" % pp])
                    TT("dve", xdtt[pp][:].rearrange("p a (h d) -> p a h d", h=2), pv,
                       dtx[:, g * 4:(g + 1) * 4, 2 * hp:2 * hp + 2].unsqueeze(3).to_broadcast([128, 4, 2, 64]), ALU.mult,
                       [pst[bi], "dtx"], ["xdtt%d" % pp])

                def _slots(hp):
                    out = []
                    for cc in range(2):
                        for hl in range(2):
                            bk = (2 * hp + hl) % 6
                            out.append((cc, hl, 2 * hp + hl, cc * 2 + hl, bcs[bk], "bc%d" % bk))
                    return out

                def stE1(g, hp):
                    slots = _slots(hp)
                    for (cc, hl, hh, sl, bcur, bcurt) in slots:
                        TS("dve", tmp[sl][:, 0:256], bcur[:, cc * 256:(cc + 1) * 256], nacum[:, g * 4 + cc * 2, hh:hh + 1], None, ALU.add, None,
                           [bcurt, "nacum"], ["tmp%d" % sl])
                        TS("dve", tmp[sl][:, 256:384], bcur[:, cc * 256 + 128:(cc + 1) * 256], nacum[:, g * 4 + cc * 2 + 1, hh:hh + 1], None,
                           ALU.add, None, [bcurt, "nacum"], ["tmp%d" % sl])
                    for (cc, hl, hh, sl, bcur, bcurt) in slots:
                        ACT(ebc[sl][:], bcur[:, cc * 256:(cc + 1) * 256], AF.Exp, [bcurt], ["ebc%d" % sl])
                    for (cc, hl, hh, sl, bcur, bcurt) in slots:
                        for q_ in (0, 256):
                            P.op("pool", "affine_select", ["tmp%d" % sl], ["tmp%d" % sl], out=tmp[sl][:, q_:q_ + 128], in_=tmp[sl][:, q_:q_ + 128],
                                 pattern=[[1, 128]], compare_op=ALU.is_ge, fill="@neg", base=0, channel_multiplier=-1)
                    for (cc, hl, hh, sl, bcur, bcurt) in slots:
                        ACT(dec[sl][:], tmp[sl][:], AF.Exp, ["tmp%d" % sl], ["dec%d" % sl])

                def stE2(g, hp):
                    gi = hp // 4
                    slots = _slots(hp)
                    for (cc, hl, hh, sl, bcur, bcurt) in slots:
                        TT("pool", Ct[sl][:], CTt[:, gi, cc * 256:(cc + 1) * 256], ebc[sl][:], ALU.mult, ["CT", "ebc%d" % sl], ["Ct%d" % sl])
                    for (cc, hl, hh, sl, bcur, bcurt) in slots:
                        TT("dve", MT[sl][:], cb[:, gi, cc, :], dec[sl][:], ALU.mult, ["cb", "dec%d" % sl], ["MT%d" % sl])

                def stB1(g, hp):
                    gi = hp // 4
                    pp = hp % 2
                    by = 4 + pp
                    bs = 6 + pp
                    hb_, hbt = hinb[pp], "hinb%d" % pp
                    for cc in range(2):
                        c = g * 2 + cc
                        for li in range(2):
                            MM(ps[bs][:, cc * 128:(cc + 1) * 128], Btok[:, gi, cc * 2 + li, :], xdtt[pp][:, cc * 2 + li, :], li == 0, li == 1,
                               ["Btok", "xdtt%d" % pp], [pst[bs]])
                        CP("dve", hb_[:, cc, :, :], hin[:, 2 * hp:2 * hp + 2, :], ["hin"], [hbt])
                        for hl in range(2):
                            hh = 2 * hp + hl
                            STT(hin[:, hh, :], hin[:, hh, :], cdec[:, c, hh:hh + 1], ps[bs][:, cc * 128 + hl * 64:cc * 128 + (hl + 1) * 64],
                                ALU.mult, ALU.add, ["hin", "cdec", pst[bs]], ["hin"])

                def stB2(g, hp):
                    pp = hp % 2
                    by = 4 + pp
                    hb_, hbt = hinb[pp], "hinb%d" % pp
                    for cc in range(2):
                        for hl in range(2):
                            sl = cc * 2 + hl
                            yo = ps[by][hl * 64:(hl + 1) * 64, cc * 256:(cc + 1) * 256]
                            MM(yo, hb_[:, cc, hl, :], Ct[sl][:], True, False, [hbt, "Ct%d" % sl], [pst[by]])
                            MM(yo, xdt[pp][:, cc * 2, hl * 64:(hl + 1) * 64], MT[sl][:, 0:256], False, False, ["xdt%d" % pp, "MT%d" % sl], [pst[by]])
                            MM(ps[by][hl * 64:(hl + 1) * 64, cc * 256 + 128:(cc + 1) * 256], xdt[pp][:, cc * 2 + 1, hl * 64:(hl + 1) * 64],
                               MT[sl][:, 256:384], False, True, ["xdt%d" % pp, "MT%d" % sl], [pst[by]])

                def stEP(g, hp):
                    gi = hp // 4
                    pp = hp % 2
                    p3 = hp % 3
                    by = 4 + pp
                    xs, xst = acc[p3], "acc%d" % p3
                    STT(yv[:], xs[:], pf("dskip", hp), ps[by][:], ALU.mult, ALU.add, [xst, "pfm", pst[by]], ["yv"])
                    TT("pool", yg[:], yv[:], szT[p3][:], ALU.mult, ["yv", "szT%d" % p3], ["yg"])
                    TT("pool", ysq2[:], yg[:], yg[:], ALU.mult, ["yg"], ["ysq2"])
                    TS("pool", ymixT[:, 8 + hp, g * 512:(g + 1) * 512], yg[:], pf("gssm", hp), None, ALU.mult, None,
                       ["yg", "pfm"], ["ymixT"])

                    def ssq_fn():
                        bi = bankA()
                        for i in range(4):
                            MM(ps[bi][:, i:i + 1], ysq2[:, i * 128:(i + 1) * 128], ones_b[:, 0:1], True, True, ["ysq2", "ones_b"], [pst[bi]])
                        TT("dve", ssq_mix[:, 1 + gi, g * 4:(g + 1) * 4], ssq_mix[:, 1 + gi, g * 4:(g + 1) * 4], ps[bi][:, 0:4],
                           ALU.add, [pst[bi], "ssq_mix"], ["ssq_mix"])
                    return ssq_fn

                wzx_issue(0)
                wzx_issue(1)
                for g in range(4):
                    BCgroup(g)
                    for k_ in range(3):
                        stA1(g, k_)
                    stA2(g, 0)
                    stA2(g, 1)
                    stE1(g, 0)
                    stE2(g, 0)
                    stE1(g, 1)
                    late = None
                    for hp in range(8):
                        stB1(g, hp)
                        stB2(g, hp)
                        if late is not None:
                            late()
                        late = stEP(g, hp)
                        if hp + 1 < 8:
                            stE2(g, hp + 1)
                        if hp + 2 < 8:
                            stA2(g, hp + 2)
                            stE1(g, hp + 2)
                        if hp + 3 < 8:
                            stA1(g, hp + 3)
                    late()
            if dbg and b == 0:
                dump(ymixT[:, 8, 0:512], 1536, ["ymixT"])
                dump(ymixT[:, 15, 1536:2048], 2048, ["ymixT"])
                if dbg > 4096:
                    for hp_ in range(8):
                        dump(ymixT[:, 8 + hp_, :], 4096 + hp_ * 2048, ["ymixT"])

            with Scope() as st3:
                x1 = sb("x1", [128, 16, 1024], F32, st3, side="right")
                for r_ in range(3):
                    ACT(rstd_mix[:, r_, :], ssq_mix[:, r_, :], AF.Sqrt, ["ssq_mix", "epsT"], ["rstd_mix"],
                        scale=1.0 / (1024 if r_ == 0 else 512), bias=epsT[:, 0:1])
                P.op("dve", "reciprocal", ["rstd_mix"], ["rstd_mix"], out=rstd_mix[:].rearrange("p a b -> p (a b)"),
                     in_=rstd_mix[:].rearrange("p a b -> p (a b)"))
                with Scope() as st:
                    wout = sb("wout", [128, 16, 1024], BF16, st)
                    tbuf = [sb("tbuf", [128, 512], F32, st) for _ in range(2)]
                    for k4 in range(4):
                        P.dma("pool", wout[:, k4 * 4:(k4 + 1) * 4, :], wout_d[:, k4 * 4:(k4 + 1) * 4, :], ("wout", k4), writes=["wout"])
                    for tt in range(16):
                        xtok = ("x1", tt)
                        P.dma("sp", x1[:, tt, :], x_d[b, tt * 128:(tt + 1) * 128, :], xtok, writes=[xtok])
                        for nch in range(2):
                            cs = slice(nch * 512, (nch + 1) * 512)
                            bks = []
                            for (k0, k1) in ((0, 8), (8, 12), (12, 16)):
                                bi = bankA()
                                bks.append(bi)
                                for k in range(k0, k1):
                                    MM(ps[bi][:], ymixT[:, k, tt * 128:(tt + 1) * 128], wout[:, k, cs], k == k0, k == k1 - 1,
                                       ["ymixT", "wout"], [pst[bi]])
                            tb = tbuf[(tt * 2 + nch) % 2]
                            tbt = "tbuf%d" % ((tt * 2 + nch) % 2)
                            ACT(tb[:], ps[bks[0]][:], AF.Identity, [pst[bks[0]], "rstd_mix"], [tbt], scale=rstd_mix[:, 0, tt:tt + 1])
                            STT(tb[:], ps[bks[1]][:], rstd_mix[:, 1, tt:tt + 1], tb[:], ALU.mult, ALU.add, [pst[bks[1]], "rstd_mix", tbt], [tbt])
                            STT(tb[:], ps[bks[2]][:], rstd_mix[:, 2, tt:tt + 1], tb[:], ALU.mult, ALU.add, [pst[bks[2]], "rstd_mix", tbt], [tbt])
                            TT("pool", tb[:], tb[:], gate1_bc[:, b, cs], ALU.mult, [tbt, "gate1_bc"], [tbt])
                            TT("pool", x1[:, tt, cs], x1[:, tt, cs], tb[:], ALU.add, [tbt, xtok], [xtok])
                if dbg and b == 0:
                    dump(x1[:, 0, 0:512], 2560, [("x1", 0)])
                st_seq.close()
                P.barrier()
                with Scope() as st:
                    def src2(tt, x1=x1):
                        return x1[:, tt, :], ("x1", tt)
                    norm_to_hT(st, b, src2, s2, 3)
                if "ffn" in phases:
                  with Scope() as st:
                    NWU = 5
                    wup = [sb("wup", [128, 8, 2, 128], BF16, st) for _ in range(NWU)]
                    wdn = [sb("wdn", [128, NJ, 128], BF16, st) for _ in range(2)]
                    actT = sb("actT", [128, NJ, 512], BF16, st)
                    rawf = [sb("rawf", [128, 514], F32, st) for _ in range(4)]
                    fhalo = sb("fhalo", [128, NJ, 2, 2], F32, st)
                    ag = [sb("ag", [128, 512], F32, st) for _ in range(2)]
                    av = [sb("av", [128, 512], F32, st) for _ in range(2)]
                    ffs = [sb("ffs", [128, 512], F32, st) for _ in range(2)]
                    P.op("pool", "memset", [], ["fhalo"], fhalo[:], 0.0)
                    ofw, _ = _PF["fw"]
                    ofb, _ = _PF["fb"]
                    def wup_issue(idx):
                        if idx < 4 * NJ:
                            wt_ = "wup%d" % (idx % NWU)
                            P.dma("pool", wup[idx % NWU][:], wup_d[idx % NJ], wt_, writes=[wt_])

                    def wdn_issue(idx):
                        if idx < 4 * 8:
                            wt_ = "wdn%d" % (idx % 2)
                            P.dma("pool", wdn[idx % 2][:], wdn_d[idx % 8], wt_, writes=[wt_])
                    for i_ in range(NWU - 1):
                        wup_issue(i_)
                    for tb_ in range(4):
                        wdn_issue(tb_ * 8)
                        wdn_issue(tb_ * 8 + 1)
                        for j in range(NJ):
                            idx = tb_ * NJ + j
                            wu = wup[idx % NWU]
                            wut = "wup%d" % (idx % NWU)
                            wup_issue(idx + NWU - 1)
                            bks = [proj(lambda k, wu=wu, gv=gv: wu[:, k, gv, :], wut, tb_) for gv in range(2)]
                            rws = [(rawf[(j * 2 + gv) % 4], "rawf%d" % ((j * 2 + gv) % 4)) for gv in range(2)]
                            accs = [(ag[j % 2], "ag%d" % (j % 2)), (av[j % 2], "av%d" % (j % 2))]
                            for gv in range(2):
                                CP("pool", rws[gv][0][:, 0:2], fhalo[:, j, gv, :], ["fhalo"], [rws[gv][1]])
                            for gv in range(2):
                                CP("act", rws[gv][0][:, 2:514], ps[bks[gv]][:], [pst[bks[gv]]], [rws[gv][1]])
                            for gv in range(2):
                                CP("pool", fhalo[:, j, gv, :], rws[gv][0][:, 512:514], [rws[gv][1]], ["fhalo"])
                            for gv in range(2):
                                ch = gv * NJ + j
                                TS("pool", accs[gv][0][:], rws[gv][0][:, 0:512], pfm[:, ofw + ch * 3:ofw + ch * 3 + 1],
                                   pfm[:, ofb + ch:ofb + ch + 1], ALU.mult, ALU.add, [rws[gv][1], "pfm"], [accs[gv][1]])
                            for gv in range(2):
                                ch = gv * NJ + j
                                for t_ in (1, 2):
                                    STT(accs[gv][0][:], rws[gv][0][:, t_:t_ + 512], pfm[:, ofw + ch * 3 + t_:ofw + ch * 3 + t_ + 1],
                                        accs[gv][0][:], ALU.mult, ALU.add, [rws[gv][1], accs[gv][1], "pfm"], [accs[gv][1]])
                            ACT(accs[0][0][:], accs[0][0][:], AF.Silu, [accs[0][1]], [accs[0][1]])
                            TT("dve", actT[:, j, :], accs[0][0][:], accs[1][0][:], ALU.mult, [accs[0][1], accs[1][1]], ["actT"])
                        for fc in range(8):
                            idx = tb_ * 8 + fc
                            wd = wdn[idx % 2]
                            wdt_ = "wdn%d" % (idx % 2)
                            bi = bankA()
                            for j in range(NJ):
                                MM(ps[bi][:], wd[:, j, :], actT[:, j, :], j == 0, j == NJ - 1, [wdt_, "actT"], [pst[bi]])
                            if fc < 6:
                                wdn_issue(idx + 2)
                            ff = ffs[fc % 2]
                            fft = "ffs%d" % (fc % 2)
                            ACT(ff[:], ps[bi][:], AF.Identity, [pst[bi], "modT"], [fft], scale=modT[:, 5, fc, b:b + 1])
                            b2 = 4 + (fc % 4)
                            for i in range(4):
                                TR(ps[b2][:, i * 128:(i + 1) * 128], ff[:, i * 128:(i + 1) * 128], ident_f[:], [fft, "ident_f"], [pst[b2]])
                            xv = x1[:, tb_ * 4:(tb_ + 1) * 4, fc * 128:(fc + 1) * 128]
                            TT("dve", xv, xv, ps[b2][:].rearrange("p (a c) -> p a c", a=4), ALU.add,
                               [pst[b2]] + [("x1", tb_ * 4 + i) for i in range(4)], [("x1", tb_ * 4 + i) for i in range(4)])
                        for i in range(4):
                            tt = tb_ * 4 + i
                            P.dma("sp", y_d[b, tt * 128:(tt + 1) * 128, :], x1[:, tt, :], ("x1", tt), reads=[("x1", tt)])
                else:
                    for tt in range(16):
                        P.dma("sp", y_d[b, tt * 128:(tt + 1) * 128, :], x1[:, tt, :], ("x1", tt), reads=[("x1", tt)])
    P.wait_all_dma("sp")
    P.emit()
    return nc


_NC_CACHE = {}


def _prep(inp):
    f = lambda a: np.ascontiguousarray(np.asarray(a, dtype=np.float32))
    fm = lambda v: f(np.asarray(v).reshape(-1, 128).T)
    w_in = np.asarray(inp["w_in"])[0]
    w_ada = np.asarray(inp["w_ada"])[0]
    sh = {}
    sh["w_ada"] = f(w_ada.reshape(8, 128, 6, 1024).transpose(2, 1, 0, 3))
    cw = np.asarray(inp["conv_ssm_w"])[0]
    fw = np.asarray(inp["conv_ffn_w"])[0]
    parts = {
        "b_ada": fm(np.asarray(inp["b_ada"])[0]),
        "g1": fm(np.asarray(inp["norm1_g"])[0]), "g2": fm(np.asarray(inp["norm2_g"])[0]),
        "gq": fm(np.asarray(inp["q_norm_g"])[0]), "gk": fm(np.asarray(inp["k_norm_g"])[0]),
        "cw": f(cw.reshape(4, 12, 128).transpose(2, 1, 0).reshape(128, 48)),
        "cb": fm(np.asarray(inp["conv_ssm_b"])[0]),
        "dskip": fm(np.repeat(np.asarray(inp["d_skip"])[0], 64)),
        "gssm": fm(np.asarray(inp["ssm_norm_g"])[0]), "gattn": fm(np.asarray(inp["attn_norm_g"])[0]),
        "fw": f(fw.reshape(3, 44, 128).transpose(2, 1, 0).reshape(128, 132)),
        "fb": fm(np.asarray(inp["conv_ffn_b"])[0]),
    }
    pfm = np.concatenate([parts[n] for n in _PF], axis=1)
    assert pfm.shape == (128, NPF)
    sh["pfm"] = f(pfm)
    sh["bgate"] = f(np.broadcast_to(np.asarray(inp["b_ada"])[0][2048:3072][None, :], (128, 1024)))
    sh["pbc"] = f(np.concatenate([np.broadcast_to(np.asarray(inp["dt_bias"])[0][None, :], (128, 16)),
                                  np.broadcast_to(np.asarray(inp["a_log"])[0][None, :], (128, 16))], axis=1))
    wk = w_in.reshape(8, 128, 5648)
    qkv = wk[:, :, 0:3072].reshape(8, 128, 3, 8, 128)
    sh["w_qkv"] = f(qkv.transpose(3, 1, 0, 2, 4))
    z = wk[:, :, 3072:4096].reshape(8, 128, 8, 128)
    xs = wk[:, :, 4096:5120].reshape(8, 128, 8, 128)
    sh["w_zx"] = f(np.stack([z, xs], axis=3).transpose(2, 1, 0, 3, 4))
    bc = wk[:, :, 5120:5632].reshape(8, 128, 4, 128)
    sh["w_bc"] = f(bc.transpose(1, 0, 2, 3))
    sh["w_dt"] = f(wk[:, :, 5632:5648].transpose(1, 0, 2))
    sh["w_out"] = f(np.asarray(inp["w_out"])[0].reshape(16, 128, 1024).transpose(1, 0, 2))
    wu = np.asarray(inp["w_up"])[0].reshape(8, 128, 2, NJ, 128)
    sh["w_up"] = f(wu.transpose(3, 1, 0, 2, 4))
    wd = np.asarray(inp["w_down"])[0].reshape(NJ, 128, 8, 128)
    sh["w_down"] = f(wd.transpose(2, 1, 0, 3))
    return sh


def kernel(**inp):
    x = np.asarray(inp["x"], dtype=np.float32)
    c = np.asarray(inp["c"], dtype=np.float32)
    sh = _prep(inp)
    if "nc" not in _NC_CACHE:
        _NC_CACHE["nc"] = build()
    nc = _NC_CACHE["nc"]
    in_maps = []
    for i in range(8):
        m = dict(sh)
        m["x"] = np.ascontiguousarray(x[2 * i:2 * i + 2])
        m["cT"] = np.ascontiguousarray(c[2 * i:2 * i + 2].T.reshape(8, 128, 2).transpose(1, 0, 2))
        in_maps.append(m)
    res = run_bass_kernel_spmd(nc, in_maps, core_ids=list(range(8)))
    return np.concatenate([r["y"] for r in res.results], axis=0).astype(np.float32)
```

```python
import numpy as np
import concourse.bass as bass
import concourse.mybir as mybir
from concourse.alu_op_type import AluOpType as ALU
from concourse.bass_utils import run_bass_kernel_spmd

F32 = mybir.dt.float32
BF16 = mybir.dt.bfloat16
AF = mybir.ActivationFunctionType
AX = mybir.AxisListType
ENGS = ["pe", "act", "dve", "pool", "sp"]

D = 1024
S = 2048
NSEQ = 2
NT = S // 128
DFF = 2816
NJ = DFF // 128
EPS = 1e-6
NEG = -30000.0


class Prog:
    def __init__(self, nc):
        self.nc = nc
        self.ops = {e: [] for e in ENGS}
        self.last_w = {}
        self.readers = {}
        self.dma_keys = {}
        self.waited = {e: {} for e in ENGS}
        self.sems = {e: nc.alloc_semaphore("sem_" + e) for e in ENGS}

    def _deps(self, eng, reads, writes):
        deps = []

        def add(ev, raw):
            if ev is None:
                return
            w = self.waited[eng]
            if ev[0] == "c":
                if ev[1] == eng and eng == "pe":
                    return
                k = ("c", ev[1])
                if w.get(k, -1) >= ev[2]:
                    return
                w[k] = ev[2]
                deps.append(ev)
                self.ops[ev[1]][ev[2]]["signal"] = True
            else:
                k = ("d", ev[1])
                if w.get(k, -1) >= ev[2]:
                    return
                w[k] = ev[2]
                deps.append(ev)

        for t in reads:
            add(self.last_w.get(t), True)
        for t in writes:
            add(self.last_w.get(t), False)
            rd = self.readers.get(t)
            if rd:
                for ev in rd.values():
                    add(ev, False)
        return deps

    def _commit(self, ev, reads, writes):
        for t in reads:
            self.readers.setdefault(t, {})[(ev[0], ev[1])] = ev
        for t in writes:
            self.last_w[t] = ev
            self.readers[t] = {}

    def op(self, eng, name, reads, writes, *args, **kw):
        deps = self._deps(eng, reads, writes)
        idx = len(self.ops[eng])
        self.ops[eng].append({"name": name, "args": args, "kw": kw, "deps": deps,
                              "signal": False, "dma": None})
        self._commit(("c", eng, idx), reads, writes)

    def dma(self, eng, out, in_, key, reads=(), writes=()):
        deps = self._deps(eng, reads, writes)
        if key not in self.dma_keys:
            self.dma_keys[key] = [self.nc.alloc_semaphore("dsem%d" % len(self.dma_keys)), 0]
        ent = self.dma_keys[key]
        ent[1] += 16
        self.ops[eng].append({"name": "dma_start", "args": (), "kw": dict(out=out, in_=in_),
                              "deps": deps, "signal": False, "dma": (ent[0], ent[1])})
        self._commit(("d", key, ent[1]), reads, writes)

    def barrier(self):
        evs = []
        for e in ENGS:
            if self.ops[e]:
                idx = len(self.ops[e]) - 1
                while idx >= 0 and self.ops[e][idx]["name"] in (None, "dma_start"):
                    idx -= 1
                if idx >= 0:
                    evs.append(("c", e, idx))
        devs = [("d", k, v[1]) for k, v in self.dma_keys.items()]
        for e in ENGS:
            deps = []
            w = self.waited[e]
            for ev in evs:
                if ev[1] == e:
                    continue
                k = ("c", ev[1])
                if w.get(k, -1) >= ev[2]:
                    continue
                w[k] = ev[2]
                deps.append(ev)
                self.ops[ev[1]][ev[2]]["signal"] = True
            for ev in devs:
                k = ("d", ev[1])
                if w.get(k, -1) >= ev[2]:
                    continue
                w[k] = ev[2]
                deps.append(ev)
            self.ops[e].append({"name": None, "deps": deps, "signal": False, "dma": None})

    def wait_all_dma(self, eng="sp"):
        deps = [("d", k, v[1]) for k, v in self.dma_keys.items()]
        self.ops[eng].append({"name": None, "deps": deps, "signal": False, "dma": None})

    def emit(self):
        nc = self.nc
        ranks = {}
        for e in ENGS:
            r = 0
            rk = []
            for o in self.ops[e]:
                if o["signal"]:
                    r += 1
                rk.append(r)
            ranks[e] = rk

        def run(e, handle):
            regs = {}
            if e == "pool":
                regs = {"@zero": handle.to_reg(0.0), "@neg": handle.to_reg(NEG)}
            for o in self.ops[e]:
                if regs and o.get("name") == "affine_select":
                    o["kw"]["fill"] = regs[o["kw"]["fill"]]
                for ev in o["deps"]:
                    if ev[0] == "c":
                        handle.wait_ge(self.sems[ev[1]], ranks[ev[1]][ev[2]])
                    else:
                        handle.wait_ge(self.dma_keys[ev[1]][0], ev[2])
                if o["name"] is None:
                    continue
                inst = getattr(handle, o["name"])(*o["args"], **o["kw"])
                if o["dma"] is not None:
                    inst.then_inc(o["dma"][0], 16)
                elif o["signal"]:
                    inst.then_inc(self.sems[e], 1)

        with nc.Block() as block:
            @block.tensor
            def _(eng):
                run("pe", eng)

            @block.scalar
            def _(eng):
                run("act", eng)

            @block.vector
            def _(eng):
                run("dve", eng)

            @block.gpsimd
            def _(eng):
                run("pool", eng)

            @block.sync
            def _(eng):
                run("sp", eng)


_PF = {}
_o = 0
for _n, _w in [("b_ada", 48), ("g1", 8), ("g2", 8), ("gq", 1), ("gk", 1), ("cw", 48), ("cb", 12),
               ("dskip", 8), ("gssm", 8), ("gattn", 8), ("fw", 132), ("fb", 44)]:
    _PF[_n] = (_o, _w)
    _o += _w
NPF = _o


def build(nseq=NSEQ, dbg=None, phases=("attn", "ssd", "out", "ffn")):
    from contextlib import ExitStack, contextmanager
    nc = bass.Bass("TRN2", target_bir_lowering=False, dynamic_dma_scratch_size=4096)
    P = Prog(nc)
    dram = lambda n, s, k="ExternalInput": nc.dram_tensor(n, s, F32, kind=k).ap()
    x_d = dram("x", [NSEQ, S, D])
    cT_d = dram("cT", [128, 8, NSEQ])
    wada_d = dram("w_ada", [6, 128, 8, 1024])
    pfm_d = dram("pfm", [128, NPF])
    bgate_d = dram("bgate", [128, 1024])
    pbc_d = dram("pbc", [128, 32])
    wqkv_d = dram("w_qkv", [8, 128, 8, 3, 128])
    wzx_d = dram("w_zx", [8, 128, 8, 2, 128])
    wbc_d = dram("w_bc", [128, 8, 4, 128])
    wdt_d = dram("w_dt", [128, 8, 16])
    wout_d = dram("w_out", [128, 16, 1024])
    wup_d = dram("w_up", [NJ, 128, 8, 2, 128])
    wdn_d = dram("w_down", [8, 128, NJ, 128])
    y_d = dram("y", [NSEQ, S, D], "ExternalOutput")
    scr_d = nc.dram_tensor("scr_acum", [16, S], F32).ap()
    dbg_d = dram("dbg", [128, dbg], "ExternalOutput") if dbg else None

    cnt = [0]

    @contextmanager
    def Scope():
        with ExitStack() as es:
            yield es
        P.barrier()

    def sb(n, s, dt=F32, st=None, side=None):
        cnt[0] += 1
        nm = "%s_%d" % (n, cnt[0])
        if st is None:
            return nc.alloc_sbuf_tensor(nm, s, dt)
        if side is not None:
            return st.enter_context(nc.sbuf_tensor(nm, s, dt, side=side))
        return st.enter_context(nc.sbuf_tensor(nm, s, dt))

    ps = [nc.alloc_psum_tensor("ps%d" % i, [128, 512], F32) for i in range(8)]
    psb = [p[:].bitcast(BF16) for p in ps]
    pst = ["ps%d" % i for i in range(8)]
    rot = {"A": 0}

    def bankA():
        i = rot["A"] % 4
        rot["A"] += 1
        return i

    def MM(out, lhsT, rhs, start, stop, r, w):
        P.op("pe", "matmul", r, w, out, lhsT=lhsT, rhs=rhs, start=start, stop=stop)

    def TR(out, in_, ident, r, w):
        P.op("pe", "transpose", r, w, out=out, in_=in_, identity=ident)

    def ACT(out, in_, func, r, w, **kw):
        P.op("act", "activation", r, w, out=out, in_=in_, func=func, **kw)

    def TT(eng, out, in0, in1, op, r, w):
        P.op(eng, "tensor_tensor", r, w, out=out, in0=in0, in1=in1, op=op)

    def TS(eng, out, in0, s1, s2, op0, op1, r, w):
        if op1 is None and eng == "pool":
            s2, op1 = 0.0, ALU.add
        if op1 is None:
            P.op(eng, "tensor_scalar", r, w, out=out, in0=in0, scalar1=s1, scalar2=None, op0=op0)
        else:
            P.op(eng, "tensor_scalar", r, w, out=out, in0=in0, scalar1=s1, scalar2=s2, op0=op0, op1=op1)

    def STT(out, in0, scalar, in1, op0, op1, r, w):
        P.op("dve", "scalar_tensor_tensor", r, w, out=out, in0=in0, scalar=scalar, in1=in1, op0=op0, op1=op1)

    def CP(eng, out, in_, r, w):
        if eng == "act":
            ACT(out, in_, AF.Copy, r, w)
        else:
            P.op(eng, "tensor_copy", r, w, out=out, in_=in_)

    def dump(ap, col, toks):
        if dbg_d is not None:
            n = 1
            for d_ in ap.shape[1:]:
                n *= d_
            P.dma("pool", dbg_d[0:ap.shape[0], col:col + n], ap, ("dbg", col), reads=toks)

    ident_f = sb("ident_f", [128, 128])
    ident_b = sb("ident_b", [128, 128], BF16)
    ones_f = sb("ones_f", [128, 128])
    ones_b = sb("ones_b", [128, 128], BF16)
    eblk = sb("eblk", [128, 8, 128], BF16)
    T0 = sb("T0", [128, 256])
    T1 = sb("T1", [128, 256])
    epsT = sb("epsT", [128, 1])
    oneT = sb("oneT", [128, 1])
    pfm = sb("pfm_sb", [128, NPF])
    pbc = sb("pbc_sb", [128, 32])
    a_bc = sb("a_bc", [128, 16])
    modT = sb("modT", [128, 6, 8, NSEQ])
    s1 = sb("s1", [128, 8, NSEQ])
    s2 = sb("s2", [128, 8, NSEQ])
    gate1_bc = sb("gate1_bc", [128, NSEQ, 1024])
    cT = sb("cT_sb", [128, 8, NSEQ])
    sc_b = sb("sc_b", [128, 8, NSEQ], BF16)
    gm = sb("gm", [128, 8, 8])
    top8 = sb("top8", [128, 8, 8])
    negm = sb("negm", [128, 8, 8])
    nmT = sb("nmT", [128, S], BF16)
    ssq_mix = sb("ssq_mix", [128, 3, 16])
    rstd_mix = sb("rstd_mix", [128, 3, 16])
    stat = sb("stat", [128, 3, 8])
    hT = sb("hT", [128, 8, S], BF16)

    def pf(name, i=0, n=1):
        o, w = _PF[name]
        return pfm[:, o + i:o + i + n]

    P.dma("sp", pfm[:], pfm_d, "pfm", writes=["pfm"])
    P.dma("sp", pbc[:], pbc_d, "pbc", writes=["pbc"])
    P.dma("sp", cT[:], cT_d, "cT", writes=["cT"])
    for b in range(NSEQ):
        P.dma("sp", gate1_bc[:, b, :], bgate_d, ("g1bc", b), writes=["gate1_bc"])
    with Scope() as st:
        big_ones = sb("big_ones", [128, 1024], BF16, st)
        ones256 = sb("ones256", [128, 256], F32, st)
        sc_rep = sb("sc_rep", [128, 8, NSEQ, 128], BF16, st)
        wa = [sb("wa", [128, 8, 1024], BF16, st) for _ in range(6)]
        P.op("pool", "memset", [], ["ones_f"], ones_f[:], 1.0)
        P.op("pool", "memset", [], ["ones_b"], ones_b[:], 1.0)
        P.op("pool", "memset", [], ["big_ones"], big_ones[:], 1.0)
        P.op("pool", "memset", [], ["ones256"], ones256[:], 1.0)
        P.op("pool", "memset", [], ["epsT"], epsT[:], EPS)
        P.op("pool", "memset", [], ["oneT"], oneT[:], 1.0)
        P.op("pool", "memset", [], ["nmT0"], nmT[:], 0.0)
        P.op("pool", "memset", [], ["gm"], gm[:], -1e30)
        P.op("pool", "affine_select", ["ones_f"], ["ident_f"], out=ident_f[:], in_=ones_f[:], pattern=[[-1, 128]],
             compare_op=ALU.is_equal, fill="@zero", base=0, channel_multiplier=1)
        P.op("pool", "affine_select", ["ones_b"], ["ident_b"], out=ident_b[:], in_=ones_b[:], pattern=[[-1, 128]],
             compare_op=ALU.is_equal, fill="@zero", base=0, channel_multiplier=1)
        P.op("pool", "affine_select", ["big_ones"], ["eblk"], out=eblk[:].rearrange("p a b -> p (a b)"),
             in_=big_ones[:], pattern=[[-1, 8], [0, 128]], compare_op=ALU.is_equal, fill="@zero", base=0,
             channel_multiplier=1)
        P.op("pool", "affine_select", ["ones256"], ["T0"], out=T0[:], in_=ones256[:], pattern=[[1, 256]],
             compare_op=ALU.is_ge, fill="@zero", base=0, channel_multiplier=-1)
        P.op("pool", "affine_select", ["ones256"], ["T1"], out=T1[:], in_=ones256[:], pattern=[[1, 256]],
             compare_op=ALU.is_ge, fill="@zero", base=-128, channel_multiplier=-1)
        ACT(a_bc[:], pbc[:, 16:32], AF.Exp, ["pbc"], ["a_bc"])
        TS("dve", a_bc[:], a_bc[:], -1.0, None, ALU.mult, None, ["a_bc"], ["a_bc"])
        ACT(cT[:], cT[:], AF.Silu, ["cT"], ["cT"])
        CP("dve", sc_b[:], cT[:], ["cT"], ["sc_b"])
        CP("dve", sc_rep[:], cT[:].unsqueeze(3).to_broadcast([128, 8, NSEQ, 128]), ["cT"], ["sc_rep"])
        ob, _ = _PF["b_ada"]
        for m in range(6):
            P.dma("pool", wa[m][:], wada_d[m], "wa%d" % m, writes=["wa%d" % m])
        for m in range(6):
            wb = wa[m]
            wt = "wa%d" % m
            if m == 2:
                for b in range(NSEQ):
                    for nch in range(2):
                        bi = bankA()
                        for k in range(8):
                            MM(ps[bi][:], sc_rep[:, k, b, :], wb[:, k, nch * 512:(nch + 1) * 512], k == 0, k == 7,
                               [wt, "sc_rep"], [pst[bi]])
                        gsl = gate1_bc[:, b, nch * 512:(nch + 1) * 512]
                        TT("dve", gsl, ps[bi][:], gsl, ALU.add, [pst[bi], "gate1_bc"], ["gate1_bc"])
            else:
                bi = bankA()
                for fc in range(8):
                    for k in range(8):
                        MM(ps[bi][:, fc * NSEQ:(fc + 1) * NSEQ], wb[:, k, fc * 128:(fc + 1) * 128], sc_b[:, k, :],
                           k == 0, k == 7, [wt, "sc_b"], [pst[bi]])
                TT("dve", modT[:, m, :, :], ps[bi][:, 0:8 * NSEQ].rearrange("p (a b) -> p a b", b=NSEQ),
                   pfm[:, ob + m * 8:ob + m * 8 + 8].unsqueeze(2).to_broadcast([128, 8, NSEQ]), ALU.add,
                   [pst[bi], "pfm"], ["modT"])
        for (sx, mi, gname) in ((s1, 1, "g1"), (s2, 4, "g2")):
            og, _ = _PF[gname]
            TS("dve", sx[:], modT[:, mi, :, :], 1.0, None, ALU.add, None, ["modT"], ["sx"])
            TT("dve", sx[:], sx[:], pfm[:, og:og + 8].unsqueeze(2).to_broadcast([128, 8, NSEQ]), ALU.mult,
               ["sx", "pfm"], ["sx"])

    nrm = {"i": 0}

    hTt = ["hT"] * 8

    def norm_to_hT(st, b, src_fn, scl, shift_m):
        xn = [sb("xn", [128, 1024], F32, st) for _ in range(4)]
        junk = [sb("junk", [128, 1024], F32, st) for _ in range(2)]
        for q4 in range(4):
            srcs = [src_fn(q4 * 4 + i) for i in range(4)]
            sl0 = (nrm["i"] % 2) * 4
            nrm["i"] += 1
            stt = ("stat", sl0)
            for i in range(4):
                ACT(junk[i % 2][:], srcs[i][0], AF.Square, [srcs[i][1]], ["junk%d" % (i % 2)])
                P.op("dve", "tensor_reduce", ["junk%d" % (i % 2)], [stt], out=stat[:, 0, sl0 + i:sl0 + i + 1], in_=junk[i % 2][:],
                     axis=AX.X, op=ALU.add)
            for i in range(4):
                ACT(stat[:, 1, sl0 + i:sl0 + i + 1], stat[:, 0, sl0 + i:sl0 + i + 1], AF.Sqrt, [stt, "epsT"], [stt], scale=1.0 / D,
                    bias=epsT[:, 0:1])
            for i in range(4):
                P.op("dve", "reciprocal", [stt], [stt], out=stat[:, 2, sl0 + i:sl0 + i + 1], in_=stat[:, 1, sl0 + i:sl0 + i + 1])
            for i in range(4):
                eng = "pool"
                TS(eng, xn[i][:], srcs[i][0], stat[:, 2, sl0 + i:sl0 + i + 1], None, ALU.mult, None, [srcs[i][1], stt], ["xn%d" % i])
            for h2 in range(2):
                hg = q4 * 2 + h2
                base = 0 if hg % 2 == 0 else 4
                for i in range(2):
                    xb_, xnt = xn[h2 * 2 + i], "xn%d" % (h2 * 2 + i)
                    for k in range(8):
                        bi = base + k // 2
                        c0 = ((k % 2) * 2 + i) * 128
                        TR(ps[bi][:, c0:c0 + 128], xb_[:, k * 128:(k + 1) * 128], ident_f[:], [xnt, "ident_f"], [pst[bi]])
                for k in range(8):
                    bi = base + k // 2
                    c0 = (k % 2) * 256
                    dst = hT[:, k, hg * 256:(hg + 1) * 256]
                    if k % 2 == 0:
                        ACT(dst, ps[bi][:, c0:c0 + 256], AF.Identity, [pst[bi], "sx", "modT"], [hTt[k]],
                            scale=scl[:, k, b:b + 1], bias=modT[:, shift_m, k, b:b + 1])
                    else:
                        TS("dve", dst, ps[bi][:, c0:c0 + 256], scl[:, k, b:b + 1], modT[:, shift_m, k, b:b + 1],
                           ALU.mult, ALU.add, [pst[bi], "sx", "modT"], [hTt[k]])

    def proj(w_ap_fn, wtok, g, extra_r=()):
        bi = bankA()
        for k in range(8):
            MM(ps[bi][:], w_ap_fn(k), hT[:, k, g * 512:(g + 1) * 512], k == 0, k == 7,
               [wtok, hTt[k]] + list(extra_r), [pst[bi]])
        return bi

    for b in range(nseq):
        st_seq = ExitStack()
        if True:
            ymixT = sb("ymixT", [128, 16, S], BF16, st_seq)
            with Scope() as st:
                xts = [sb("xt", [128, 1024], F32, st) for _ in range(4)]

                def src1(tt, xts=xts, b=b):
                    t = xts[tt % 4]
                    tok = "xt%d" % (tt % 4)
                    P.dma("sp", t[:], x_d[b, tt * 128:(tt + 1) * 128, :], tok, writes=[tok])
                    return t[:], tok
                norm_to_hT(st, b, src1, s1, 0)
            P.op("pool", "memset", [], ["ssq_mix"], ssq_mix[:], 0.0)
            if dbg and b == 0:
                dump(hT[:, 0, 0:512], 0, [hTt[0]])

            if "attn" in phases:
              with Scope() as st:
                wq = [sb("wqkv", [128, 8, 3, 128], BF16, st) for _ in range(2)]
                qf = [sb("qf", [128, 512], F32, st) for _ in range(2)]
                sq = [sb("sq", [128, 512], BF16, st) for _ in range(2)]
                lnv = sb("lnv", [128, 512], F32, st)
                rs = sb("rs", [128, 512], F32, st)
                qT = [sb("qT", [128, S], BF16, st) for _ in range(2)]
                kT = [sb("kT", [128, S], BF16, st) for _ in range(2)]
                vT = sb("vT", [128, S], BF16, st)
                vtok = [sb("vtok", [128, 16, 128], BF16, st) for _ in range(2)]
                nm2 = [nmT, sb("nmT2", [128, S], BF16, st)]
                PT = [sb("PT", [128, 512], BF16, st) for _ in range(4)]
                lnd = sb("lnd", [128, 512], F32, st)
                rd = sb("rd", [128, 512], F32, st)
                yf = sb("yf", [128, 512], F32, st)
                ysq = sb("ysq", [128, 512], BF16, st)
                kmf = sb("kmf", [128, 8], F32, st)
                kmb = sb("kmb", [128, 8], BF16, st)
                P.op("pool", "memset", [], ["nmT1"], nm2[1][:], 0.0)
                ptc = [0]
                uct = [0]

                def wq_issue(h):
                    if h < 8:
                        P.dma("pool", wq[h % 2][:], wqkv_d[h], "wqkv%d" % (h % 2), writes=["wqkv%d" % (h % 2)])

                def proj_units(h):
                    hb = h % 2
                    wtk = "wqkv%d" % hb
                    qTt, kTt, vtt, nmt = "qT%d" % hb, "kT%d" % hb, "vtok%d" % hb, "nmT%d" % hb
                    units = []
                    for ti, (dstT, dtok, gn) in enumerate(((qT[hb], qTt, "gq"), (kT[hb], kTt, "gk"))):
                        for g in range(4):
                            sl = (ti * 4 + g) % 2

                            def u1(ti=ti, g=g, sl=sl):
                                bi = proj(lambda k: wq[hb][:, k, ti, :], wtk, g)
                                CP("dve", qf[sl][:], ps[bi][:], [pst[bi]], ["qf%d" % sl])
                                TT("pool", sq[sl][:], qf[sl][:], qf[sl][:], ALU.mult, ["qf%d" % sl], ["sq%d" % sl])

                            def u2(g=g, sl=sl, dstT=dstT, dtok=dtok, gn=gn):
                                b2 = bankA()
                                MM(ps[b2][:], ones_b[:], sq[sl][:], True, True, ["ones_b", "sq%d" % sl], [pst[b2]])
                                ACT(lnv[:], ps[b2][:], AF.Ln, [pst[b2], "epsT"], ["lnv"], scale=1.0 / 128, bias=epsT[:, 0:1])
                                ACT(rs[:], lnv[:], AF.Exp, ["lnv"], ["rs"], scale=-0.5)
                                STT(dstT[:, g * 512:(g + 1) * 512], qf[sl][:], pf(gn), rs[:], ALU.mult, ALU.mult,
                                    ["qf%d" % sl, "rs", "pfm"], [dtok])
                            units += [u1, u2]
                    u1s, u2s = units[0::2], units[1::2]
                    units = [u1s[0]]
                    for k_ in range(1, 8):
                        units += [u1s[k_], u2s[k_ - 1]]
                    units.append(u2s[7])
                    for g in range(4):
                        def uv1(g=g):
                            bi = proj(lambda k: wq[hb][:, k, 2, :], wtk, g)
                            CP("act", vT[:, g * 512:(g + 1) * 512], ps[bi][:], [pst[bi]], ["vT"])

                        def uv2(g=g):
                            bi = bankA()
                            for i in range(4):
                                tt = g * 4 + i
                                TR(psb[bi][:, i * 128:(i + 1) * 128], vT[:, tt * 128:(tt + 1) * 128], ident_b[:],
                                   ["vT", "ident_b"], [pst[bi]])
                            CP("dve", vtok[hb][:, g * 4:(g + 1) * 4, :].rearrange("p a b -> p (a b)"), psb[bi][:, 0:512],
                               [pst[bi]], [vtt])
                        units += [uv1, uv2]

                    def ug1():
                        P.op("dve", "tensor_reduce", [kTt], ["kmf"], out=kmf[:], in_=kT[hb][:].rearrange("p (n t) -> p n t", t=256),
                             axis=AX.X, op=ALU.add)
                        CP("dve", kmb[:], kmf[:], ["kmf"], ["kmb"])

                    gst = {}

                    def ug2():
                        bi = bankA()
                        gst["bi"] = bi
                        for i in range(8):
                            tt = 8 + i
                            MM(ps[bi][:, i * 8:(i + 1) * 8], qT[hb][:, tt * 128:(tt + 1) * 128], kmb[:], True, True,
                               [qTt, "kmb"], [pst[bi]])
                        for i in range(8):
                            own = (8 + i) // 2
                            CP("dve", gm[:, i, 0:own], ps[bi][:, i * 8:i * 8 + own], [pst[bi], "gm"], ["gm"])
                        for i in range(8):
                            P.op("dve", "max", ["gm"], ["top8"], out=top8[:, i, :], in_=gm[:, i, :])
                        TT("dve", negm[:], gm[:], top8[:, :, 2:3].to_broadcast([128, 8, 8]), ALU.is_lt, ["gm", "top8"], ["negm"])
                        TS("dve", negm[:], negm[:], NEG, None, ALU.mult, None, ["negm"], ["negm"])

                    def ug3():
                        for g2 in range(2):
                            bi = bankA()
                            for i in range(4):
                                TR(ps[bi][0:8, i * 128:(i + 1) * 128], negm[:, g2 * 4 + i, :], ident_f[:], ["negm", "ident_f"],
                                   [pst[bi]])
                            CP("dve", nm2[hb][0:8, 1024 + g2 * 512:1024 + (g2 + 1) * 512], ps[bi][0:8, :], [pst[bi]], [nmt])
                    units += [ug1, ug2, ug3]
                    return units

                def main_head(h, side):
                    hb = h % 2
                    qTt, kTt, vtt, nmt = "qT%d" % hb, "kT%d" % hb, "vtok%d" % hb, "nmT%d" % hb

                    def stage1(j, kt):
                        blk = kt // 2
                        A, Bk = 2 * j, 2 * j + 1
                        c0 = 256 if blk == Bk else 0
                        bi = bankA()
                        need_mask = (j >= 2) and (blk < Bk)
                        MM(ps[bi][:, c0:512], kT[hb][:, kt * 128:(kt + 1) * 128], qT[hb][:, j * 512 + c0:(j + 1) * 512],
                           True, not need_mask, [kTt, qTt], [pst[bi]])
                        if need_mask:
                            m0 = 256 if blk == A else 0
                            MM(ps[bi][:, m0:512], eblk[:, blk, :], nm2[hb][:, j * 512 + m0:(j + 1) * 512], False, True,
                               ["eblk", nmt], [pst[bi]])
                        pt = PT[ptc[0] % 4]
                        ptk = "PT%d" % (ptc[0] % 4)
                        ptc[0] += 1
                        ACT(pt[:, c0:512], ps[bi][:, c0:512], AF.Exp, [pst[bi]], [ptk], scale=128 ** -0.5)
                        if blk >= A:
                            r = kt % 2
                            P.op("pool", "affine_select", [ptk], [ptk], out=pt[:, c0:c0 + 256], in_=pt[:, c0:c0 + 256],
                                 pattern=[[1, 256]], compare_op=ALU.is_ge, fill="@zero", base=-128 * r,
                                 channel_multiplier=-1)
                        return (j, kt, c0, pt, ptk)

                    def stage2(ctx):
                        j, kt, c0, pt, ptk = ctx
                        bo, bd = 4 + (j % 2) * 2, 5 + (j % 2) * 2
                        nk = 4 * (j + 1)
                        MM(ps[bo][:, c0:512], vtok[hb][:, kt, :], pt[:, c0:512], kt == 0, kt == nk - 1,
                           [vtt, ptk], [pst[bo]])
                        MM(ps[bd][:, c0:512], ones_b[:], pt[:, c0:512], kt == 0, kt == nk - 1,
                           ["ones_b", ptk], [pst[bd]])
                        if kt == nk - 1:
                            ACT(lnd[:], ps[bd][:], AF.Ln, [pst[bd]], ["lnd"])
                            ACT(rd[:], lnd[:], AF.Exp, ["lnd"], ["rd"], scale=-1.0)
                            TT("dve", yf[:], ps[bo][:], rd[:], ALU.mult, [pst[bo], "rd"], ["yf"])
                            TT("pool", ysq[:], yf[:], yf[:], ALU.mult, ["yf"], ["ysq"])
                            TS("pool", ymixT[:, h, j * 512:(j + 1) * 512], yf[:], pf("gattn", h), None, ALU.mult, None,
                               ["yf", "pfm"], ["ymixT"])
                            def ssq_fn(j=j):
                                bi = bankA()
                                for i in range(4):
                                    MM(ps[bi][:, i:i + 1], ysq[:, i * 128:(i + 1) * 128], ones_b[:, 0:1], True, True,
                                       ["ysq", "ones_b"], [pst[bi]])
                                TT("dve", ssq_mix[:, 0, j * 4:(j + 1) * 4], ssq_mix[:, 0, j * 4:(j + 1) * 4], ps[bi][:, 0:4],
                                   ALU.add, [pst[bi], "ssq_mix"], ["ssq_mix"])
                            defer.append([4, ssq_fn])

                    tiles = [(j, kt) for j in range(4) for kt in range(4 * (j + 1))]
                    pend = []
                    defer = []
                    for (j, kt) in tiles:
                        pend.append(stage1(j, kt))
                        if len(pend) > 3:
                            stage2(pend.pop(0))
                        for d_ in list(defer):
                            d_[0] -= 1
                            if d_[0] <= 0:
                                defer.remove(d_)
                                d_[1]()
                        if side:
                            side.pop(0)()
                    while pend:
                        stage2(pend.pop(0))
                    while side:
                        side.pop(0)()
                    for d_ in defer:
                        d_[1]()

                wq_issue(0)
                wq_issue(1)
                for u in proj_units(0):
                    u()
                for h in range(8):
                    side = proj_units(h + 1) if h + 1 < 8 else []
                    main_head(h, side)
                    wq_issue(h + 2)
            if dbg and b == 0:
                dump(ymixT[:, 0, 0:512], 512, ["ymixT"])
                dump(ymixT[:, 7, 1536:2048], 1024, ["ymixT"])

            if "ssd" in phases:
              with Scope() as st:
                wdt = sb("wdt", [128, 8, 16], BF16, st)
                wbc = sb("wbc", [128, 8, 4, 128], BF16, st)
                wzx = [sb("wzx", [128, 8, 2, 128], BF16, st) for _ in range(3)]
                dtr = sb("dtr", [128, 16, 16], F32, st)
                dtl = sb("dtl", [128, 16, 16], F32, st)
                dt_tok = sb("dt_tok", [128, 16, 16], F32, st)
                da_tok = sb("da_tok", [128, 16, 16], F32, st)
                nacum = sb("nacum", [128, 16, 16], F32, st)
                tot = sb("tot", [128, 8, 16], F32, st)
                cdec = sb("cdec", [128, 8, 16], F32, st)
                dtx = sb("dtx", [128, 16, 16], F32, st)
                acs = sb("acs", [16, 512], F32, st)
                raw = [sb("raw", [128, 515], F32, st) for _ in range(2)]
                halo = sb("halo", [128, 12, 3], F32, st)
                acc = [sb("acc", [128, 512], F32, st) for _ in range(3)]
                szT = [sb("szT", [128, 512], F32, st) for _ in range(3)]
                BT = sb("BT", [128, 2, 512], BF16, st)
                CTt = sb("CT", [128, 2, 512], BF16, st)
                Btok = sb("Btok", [128, 2, 4, 128], BF16, st)
                cb = sb("cb", [128, 2, 2, 384], BF16, st)
                xdt = [sb("xdt", [128, 4, 128], BF16, st) for _ in range(2)]
                xdtt = [sb("xdtt", [128, 4, 128], BF16, st) for _ in range(2)]
                bcs = [sb("bc", [128, 512], F32, st) for _ in range(6)]
                tmp = [sb("tmp", [128, 384], F32, st) for _ in range(4)]
                dec = [sb("dec", [128, 384], BF16, st) for _ in range(4)]
                MT = [sb("MT", [128, 384], BF16, st) for _ in range(4)]
                ebc = [sb("ebc", [128, 256], BF16, st) for _ in range(4)]
                Ct = [sb("Ct", [128, 256], BF16, st) for _ in range(4)]
                hin = sb("hin", [128, 16, 64], F32, st)
                hinb = [sb("hinb", [128, 2, 2, 64], BF16, st) for _ in range(2)]
                yv = sb("yv", [128, 512], F32, st)
                yg = sb("yg", [128, 512], F32, st)
                ysq2 = sb("ysq2", [128, 512], BF16, st)
                P.dma("pool", wdt[:], wdt_d, "wdt", writes=["wdt"])
                P.dma("pool", wbc[:], wbc_d, "wbc", writes=["wbc"])
                P.op("pool", "memset", [], ["hin"], hin[:], 0.0)
                P.op("pool", "memset", [], ["halo"], halo[:], 0.0)
                bi = bankA()
                for tt in range(16):
                    for k in range(8):
                        MM(ps[bi][:, tt * 16:(tt + 1) * 16], hT[:, k, tt * 128:(tt + 1) * 128], wdt[:, k, :], k == 0, k == 7,
                           [hTt[k], "wdt"], [pst[bi]])
                TT("dve", dtr[:], ps[bi][:, 0:256].rearrange("p (a b) -> p a b", b=16),
                   pbc[:, 0:16].unsqueeze(1).to_broadcast([128, 16, 16]), ALU.add, [pst[bi], "pbc"], ["dtr"])
                STT(dtl[:], dtr[:], -1.0, dtr[:], ALU.mult, ALU.max, ["dtr"], ["dtl"])
                ACT(dtl[:], dtl[:], AF.Exp, ["dtl"], ["dtl"], scale=-1.0)
                ACT(dtl[:], dtl[:], AF.Ln, ["dtl", "oneT"], ["dtl"], bias=oneT[:, 0:1])
                STT(dt_tok[:], dtr[:], 0.0, dtl[:], ALU.max, ALU.add, ["dtr", "dtl"], ["dt_tok"])
                TT("dve", da_tok[:], dt_tok[:], a_bc[:].unsqueeze(1).to_broadcast([128, 16, 16]), ALU.mult,
                   ["dt_tok", "a_bc"], ["da_tok"])
                b1, b2, b3 = bankA(), bankA(), bankA()
                for c in range(8):
                    t0_, t1_ = 2 * c, 2 * c + 1
                    MM(ps[b1][:, t0_ * 16:(t0_ + 1) * 16], T0[:, 0:128], da_tok[:, t0_, :], True, True, ["T0", "da_tok"], [pst[b1]])
                    MM(ps[b1][:, t1_ * 16:(t1_ + 1) * 16], T0[:, 128:256], da_tok[:, t0_, :], True, False, ["T0", "da_tok"], [pst[b1]])
                    MM(ps[b1][:, t1_ * 16:(t1_ + 1) * 16], T1[:, 128:256], da_tok[:, t1_, :], False, True, ["T1", "da_tok"], [pst[b1]])
                    MM(ps[b2][:, c * 16:(c + 1) * 16], ones_f[:], da_tok[:, t0_, :], True, False, ["ones_f", "da_tok"], [pst[b2]])
                    MM(ps[b2][:, c * 16:(c + 1) * 16], ones_f[:], da_tok[:, t1_, :], False, True, ["ones_f", "da_tok"], [pst[b2]])
                TS("dve", nacum[:].rearrange("p a b -> p (a b)"), ps[b1][:, 0:256], -1.0, None, ALU.mult, None, [pst[b1]], ["nacum"])
                CP("dve", tot[:].rearrange("p a b -> p (a b)"), ps[b2][:, 0:128], [pst[b2]], ["tot"])
                ACT(cdec[:], tot[:], AF.Exp, ["tot"], ["cdec"])
                TT("dve", dtx[:].rearrange("p (c t) h -> p c t h", t=2), nacum[:].rearrange("p (c t) h -> p c t h", t=2),
                   tot[:].unsqueeze(2).to_broadcast([128, 8, 2, 16]), ALU.add, ["nacum", "tot"], ["dtx"])
                ACT(dtx[:], dtx[:], AF.Exp, ["dtx"], ["dtx"])
                TT("dve", dtx[:], dtx[:], dt_tok[:], ALU.mult, ["dtx", "dt_tok"], ["dtx"])
                for g in range(4):
                    for cc in range(2):
                        c = g * 2 + cc
                        MM(ps[b3][0:16, cc * 256:(cc + 1) * 256], da_tok[:, 2 * c, :], T0[:], True, False, ["T0", "da_tok"], [pst[b3]])
                        MM(ps[b3][0:16, cc * 256:(cc + 1) * 256], da_tok[:, 2 * c + 1, :], T1[:], False, True, ["T1", "da_tok"], [pst[b3]])
                    CP("dve", acs[:], ps[b3][0:16, :], [pst[b3]], ["acs"])
                    P.dma("sp", scr_d[:, g * 512:(g + 1) * 512], acs[:], "acs", reads=["acs"], writes=[("scr", g)])

                def conv_silu(bi, cc, rw, rwt, out_ap, wtoks, g, a_, at):
                    oc, _ = _PF["cw"]
                    obb, _ = _PF["cb"]
                    CP("pool", rw[:, 0:3], halo[:, cc, :], ["halo"], [rwt])
                    CP("act", rw[:, 3:515], ps[bi][:], [pst[bi]], [rwt])
                    CP("pool", halo[:, cc, :], rw[:, 512:515], [rwt], ["halo"])
                    ACT(a_[:], rw[:, 0:512], AF.Identity, [rwt, "pfm"], [at], scale=pfm[:, oc + cc * 4:oc + cc * 4 + 1],
                        bias=pfm[:, obb + cc:obb + cc + 1])
                    for j_ in range(1, 4):
                        STT(a_[:], rw[:, j_:j_ + 512], pfm[:, oc + cc * 4 + j_:oc + cc * 4 + j_ + 1], a_[:], ALU.mult, ALU.add,
                            [rwt, at, "pfm"], [at])
                    ACT(out_ap, a_[:], AF.Silu, [at], wtoks)

                def wzx_issue(idx):
                    if idx < 32:
                        wt_ = "wzx%d" % (idx % 3)
                        P.dma("pool", wzx[idx % 3][:], wzx_d[idx % 8], wt_, writes=[wt_])

                def BCgroup(g):
                    for idx in range(4):
                        cc = 8 + idx
                        bi = proj(lambda k, idx=idx: wbc[:, k, idx, :], "wbc", g)
                        dst = BT[:, idx, :] if idx < 2 else CTt[:, idx - 2, :]
                        conv_silu(bi, cc, raw[idx % 2], "raw%d" % (idx % 2), dst, ["BT"] if idx < 2 else ["CT"], g,
                                  (yv, yg)[idx % 2], ("yv", "yg")[idx % 2])
                    for gi in range(2):
                        bi = bankA()
                        for i in range(4):
                            TR(psb[bi][:, i * 128:(i + 1) * 128], BT[:, gi, i * 128:(i + 1) * 128], ident_b[:], ["BT", "ident_b"], [pst[bi]])
                        CP("dve", Btok[:, gi, :, :].rearrange("p a b -> p (a b)"), psb[bi][:, 0:512], [pst[bi]], ["Btok"])
                        for cc in range(2):
                            bi = bankA()
                            MM(ps[bi][:, 0:256], BT[:, gi, cc * 256:cc * 256 + 128], CTt[:, gi, cc * 256:(cc + 1) * 256], True, True,
                               ["BT", "CT"], [pst[bi]])
                            MM(ps[bi][:, 256:384], BT[:, gi, cc * 256 + 128:cc * 256 + 256], CTt[:, gi, cc * 256 + 128:(cc + 1) * 256],
                               True, True, ["BT", "CT"], [pst[bi]])
                            CP("act", cb[:, gi, cc, :], ps[bi][:, 0:384], [pst[bi]], ["cb"])

                def stA1(g, hp):
                    idx = g * 8 + hp
                    wb_ = wzx[idx % 3]
                    wt_ = "wzx%d" % (idx % 3)
                    wzx_issue(idx + 2)
                    p3 = hp % 3
                    for hl in range(2):
                        hh = 2 * hp + hl
                        bk = (2 * hp + hl) % 6
                        P.dma("sp", bcs[bk][:], scr_d[hh:hh + 1, g * 512:(g + 1) * 512].partition_broadcast(128), "bc%d" % bk,
                              reads=[("scr", g)], writes=["bc%d" % bk])
                    bi = proj(lambda k: wb_[:, k, 0, :], wt_, g)
                    ACT(szT[p3][:], ps[bi][:], AF.Silu, [pst[bi]], ["szT%d" % p3])
                    bi = proj(lambda k: wb_[:, k, 1, :], wt_, g)
                    xs, xst = acc[p3], "acc%d" % p3
                    conv_silu(bi, hp, raw[hp % 2], "raw%d" % (hp % 2), xs[:], [xst], g, xs, xst)

                def stA2(g, hp):
                    pp = hp % 2
                    xs, xst = acc[hp % 3], "acc%d" % (hp % 3)
                    bi = bankA()
                    for i in range(4):
                        TR(ps[bi][:, i * 128:(i + 1) * 128], xs[:, i * 128:(i + 1) * 128], ident_f[:], [xst, "ident_f"], [pst[bi]])
                    pv = ps[bi][:].rearrange("p (a h d) -> p a h d", a=4, h=2)
                    TT("dve", xdt[pp][:].rearrange("p a (h d) -> p a h d", h=2), pv,
                       dt_tok[:, g * 4:(g + 1) * 4, 2 * hp:2 * hp + 2].unsqueeze(3).to_broadcast([128, 4, 2, 64]), ALU.mult,
                       [pst[bi], "dt_tok"], ["xdt%d" % pp])
                    TT("dve", xdtt[pp][:].rearrange("p a (h d) -> p a h d", h=2), pv,
                       dtx[:, g * 4:(g + 1) * 4, 2 * hp:2 * hp + 2].unsqueeze(3).to_broadcast([128, 4, 2, 64]), ALU.mult,
                       [pst[bi], "dtx"], ["xdtt%d" % pp])

                def _slots(hp):
                    out = []
                    for cc in range(2):
                        for hl in range(2):
                            bk = (2 * hp + hl) % 6
                            out.append((cc, hl, 2 * hp + hl, cc * 2 + hl, bcs[bk], "bc%d" % bk))
                    return out

                def stE1(g, hp):
                    slots = _slots(hp)
                    for (cc, hl, hh, sl, bcur, bcurt) in slots:
                        TS("dve", tmp[sl][:, 0:256], bcur[:, cc * 256:(cc + 1) * 256], nacum[:, g * 4 + cc * 2, hh:hh + 1], None, ALU.add, None,
                           [bcurt, "nacum"], ["tmp%d" % sl])
                        TS("dve", tmp[sl][:, 256:384], bcur[:, cc * 256 + 128:(cc + 1) * 256], nacum[:, g * 4 + cc * 2 + 1, hh:hh + 1], None,
                           ALU.add, None, [bcurt, "nacum"], ["tmp%d" % sl])
                    for (cc, hl, hh, sl, bcur, bcurt) in slots:
                        ACT(ebc[sl][:], bcur[:, cc * 256:(cc + 1) * 256], AF.Exp, [bcurt], ["ebc%d" % sl])
                    for (cc, hl, hh, sl, bcur, bcurt) in slots:
                        for q_ in (0, 256):
                            P.op("pool", "affine_select", ["tmp%d" % sl], ["tmp%d" % sl], out=tmp[sl][:, q_:q_ + 128], in_=tmp[sl][:, q_:q_ + 128],
                                 pattern=[[1, 128]], compare_op=ALU.is_ge, fill="@neg", base=0, channel_multiplier=-1)
                    for (cc, hl, hh, sl, bcur, bcurt) in slots:
                        ACT(dec[sl][:], tmp[sl][:], AF.Exp, ["tmp%d" % sl], ["dec%d" % sl])

                def stE2(g, hp):
                    gi = hp // 4
                    slots = _slots(hp)
                    for (cc, hl, hh, sl, bcur, bcurt) in slots:
                        TT("pool", Ct[sl][:], CTt[:, gi, cc * 256:(cc + 1) * 256], ebc[sl][:], ALU.mult, ["CT", "ebc%d" % sl], ["Ct%d" % sl])
                    for (cc, hl, hh, sl, bcur, bcurt) in slots:
                        TT("dve", MT[sl][:], cb[:, gi, cc, :], dec[sl][:], ALU.mult, ["cb", "dec%d" % sl], ["MT%d" % sl])

                def stB1(g, hp):
                    gi = hp // 4
                    pp = hp % 2
                    by = 4 + pp
                    bs = 6 + pp
                    hb_, hbt = hinb[pp], "hinb%d" % pp
                    for cc in range(2):
                        c = g * 2 + cc
                        for li in range(2):
                            MM(ps[bs][:, cc * 128:(cc + 1) * 128], Btok[:, gi, cc * 2 + li, :], xdtt[pp][:, cc * 2 + li, :], li == 0, li == 1,
                               ["Btok", "xdtt%d" % pp], [pst[bs]])
                        CP("dve", hb_[:, cc, :, :], hin[:, 2 * hp:2 * hp + 2, :], ["hin"], [hbt])
                        for hl in range(2):
                            hh = 2 * hp + hl
                            STT(hin[:, hh, :], hin[:, hh, :], cdec[:, c, hh:hh + 1], ps[bs][:, cc * 128 + hl * 64:cc * 128 + (hl + 1) * 64],
                                ALU.mult, ALU.add, ["hin", "cdec", pst[bs]], ["hin"])

                def stB2(g, hp):
                    pp = hp % 2
                    by = 4 + pp
                    hb_, hbt = hinb[pp], "hinb%d" % pp
                    for cc in range(2):
                        for hl in range(2):
                            sl = cc * 2 + hl
                            yo = ps[by][hl * 64:(hl + 1) * 64, cc * 256:(cc + 1) * 256]
                            MM(yo, hb_[:, cc, hl, :], Ct[sl][:], True, False, [hbt, "Ct%d" % sl], [pst[by]])
                            MM(yo, xdt[pp][:, cc * 2, hl * 64:(hl + 1) * 64], MT[sl][:, 0:256], False, False, ["xdt%d" % pp, "MT%d" % sl], [pst[by]])
                            MM(ps[by][hl * 64:(hl + 1) * 64, cc * 256 + 128:(cc + 1) * 256], xdt[pp][:, cc * 2 + 1, hl * 64:(hl + 1) * 64],
                               MT[sl][:, 256:384], False, True, ["xdt%d" % pp, "MT%d" % sl], [pst[by]])

                def stEP(g, hp):
                    gi = hp // 4
                    pp = hp % 2
                    p3 = hp % 3
                    by = 4 + pp
                    xs, xst = acc[p3], "acc%d" % p3
                    STT(yv[:], xs[:], pf("dskip", hp), ps[by][:], ALU.mult, ALU.add, [xst, "pfm", pst[by]], ["yv"])
                    TT("pool", yg[:], yv[:], szT[p3][:], ALU.mult, ["yv", "szT%d" % p3], ["yg"])
                    TT("pool", ysq2[:], yg[:], yg[:], ALU.mult, ["yg"], ["ysq2"])
                    TS("pool", ymixT[:, 8 + hp, g * 512:(g + 1) * 512], yg[:], pf("gssm", hp), None, ALU.mult, None,
                       ["yg", "pfm"], ["ymixT"])

                    def ssq_fn():
                        bi = bankA()
                        for i in range(4):
                            MM(ps[bi][:, i:i + 1], ysq2[:, i * 128:(i + 1) * 128], ones_b[:, 0:1], True, True, ["ysq2", "ones_b"], [pst[bi]])
                        TT("dve", ssq_mix[:, 1 + gi, g * 4:(g + 1) * 4], ssq_mix[:, 1 + gi, g * 4:(g + 1) * 4], ps[bi][:, 0:4],
                           ALU.add, [pst[bi], "ssq_mix"], ["ssq_mix"])
                    return ssq_fn

                wzx_issue(0)
                wzx_issue(1)
                for g in range(4):
                    BCgroup(g)
                    for k_ in range(3):
                        stA1(g, k_)
                    stA2(g, 0)
                    stA2(g, 1)
                    stE1(g, 0)
                    stE2(g, 0)
                    stE1(g, 1)
                    late = None
                    for hp in range(8):
                        stB1(g, hp)
                        stB2(g, hp)
                        if late is not None:
                            late()
                        late = stEP(g, hp)
                        if hp + 1 < 8:
                            stE2(g, hp + 1)
                        if hp + 2 < 8:
                            stA2(g, hp + 2)
                            stE1(g, hp + 2)
                        if hp + 3 < 8:
                            stA1(g, hp + 3)
                    late()
            if dbg and b == 0:
                dump(ymixT[:, 8, 0:512], 1536, ["ymixT"])
                dump(ymixT[:, 15, 1536:2048], 2048, ["ymixT"])
                if dbg > 4096:
                    for hp_ in range(8):
                        dump(ymixT[:, 8 + hp_, :], 4096 + hp_ * 2048, ["ymixT"])

            with Scope() as st3:
                x1 = sb("x1", [128, 16, 1024], F32, st3, side="right")
                for r_ in range(3):
                    ACT(rstd_mix[:, r_, :], ssq_mix[:, r_, :], AF.Sqrt, ["ssq_mix", "epsT"], ["rstd_mix"],
                        scale=1.0 / (1024 if r_ == 0 else 512), bias=epsT[:, 0:1])
                P.op("dve", "reciprocal", ["rstd_mix"], ["rstd_mix"], out=rstd_mix[:].rearrange("p a b -> p (a b)"),
                     in_=rstd_mix[:].rearrange("p a b -> p (a b)"))
                with Scope() as st:
                    wout = sb("wout", [128, 16, 1024], BF16, st)
                    tbuf = [sb("tbuf", [128, 512], F32, st) for _ in range(2)]
                    for k4 in range(4):
                        P.dma("pool", wout[:, k4 * 4:(k4 + 1) * 4, :], wout_d[:, k4 * 4:(k4 + 1) * 4, :], ("wout", k4), writes=["wout"])
                    for tt in range(16):
                        xtok = ("x1", tt)
                        P.dma("sp", x1[:, tt, :], x_d[b, tt * 128:(tt + 1) * 128, :], xtok, writes=[xtok])
                        for nch in range(2):
                            cs = slice(nch * 512, (nch + 1) * 512)
                            bks = []
                            for (k0, k1) in ((0, 8), (8, 12), (12, 16)):
                                bi = bankA()
                                bks.append(bi)
                                for k in range(k0, k1):
                                    MM(ps[bi][:], ymixT[:, k, tt * 128:(tt + 1) * 128], wout[:, k, cs], k == k0, k == k1 - 1,
                                       ["ymixT", "wout"], [pst[bi]])
                            tb = tbuf[(tt * 2 + nch) % 2]
                            tbt = "tbuf%d" % ((tt * 2 + nch) % 2)
                            ACT(tb[:], ps[bks[0]][:], AF.Identity, [pst[bks[0]], "rstd_mix"], [tbt], scale=rstd_mix[:, 0, tt:tt + 1])
                            STT(tb[:], ps[bks[1]][:], rstd_mix[:, 1, tt:tt + 1], tb[:], ALU.mult, ALU.add, [pst[bks[1]], "rstd_mix", tbt], [tbt])
                            STT(tb[:], ps[bks[2]][:], rstd_mix[:, 2, tt:tt + 1], tb[:], ALU.mult, ALU.add, [pst[bks[2]], "rstd_mix", tbt], [tbt])
                            TT("pool", tb[:], tb[:], gate1_bc[:, b, cs], ALU.mult, [tbt, "gate1_bc"], [tbt])
                            TT("pool", x1[:, tt, cs], x1[:, tt, cs], tb[:], ALU.add, [tbt, xtok], [xtok])
                if dbg and b == 0:
                    dump(x1[:, 0, 0:512], 2560, [("x1", 0)])
                st_seq.close()
                P.barrier()
                with Scope() as st:
                    def src2(tt, x1=x1):
                        return x1[:, tt, :], ("x1", tt)
                    norm_to_hT(st, b, src2, s2, 3)
                if "ffn" in phases:
                  with Scope() as st:
                    NWU = 5
                    wup = [sb("wup", [128, 8, 2, 128], BF16, st) for _ in range(NWU)]
                    wdn = [sb("wdn", [128, NJ, 128], BF16, st) for _ in range(2)]
                    actT = sb("actT", [128, NJ, 512], BF16, st)
                    rawf = [sb("rawf", [128, 514], F32, st) for _ in range(4)]
                    fhalo = sb("fhalo", [128, NJ, 2, 2], F32, st)
                    ag = [sb("ag", [128, 512], F32, st) for _ in range(2)]
                    av = [sb("av", [128, 512], F32, st) for _ in range(2)]
                    ffs = [sb("ffs", [128, 512], F32, st) for _ in range(2)]
                    P.op("pool", "memset", [], ["fhalo"], fhalo[:], 0.0)
                    ofw, _ = _PF["fw"]
                    ofb, _ = _PF["fb"]
                    def wup_issue(idx):
                        if idx < 4 * NJ:
                            wt_ = "wup%d" % (idx % NWU)
                            P.dma("pool", wup[idx % NWU][:], wup_d[idx % NJ], wt_, writes=[wt_])

                    def wdn_issue(idx):
                        if idx < 4 * 8:
                            wt_ = "wdn%d" % (idx % 2)
                            P.dma("pool", wdn[idx % 2][:], wdn_d[idx % 8], wt_, writes=[wt_])
                    for i_ in range(NWU - 1):
                        wup_issue(i_)
                    for tb_ in range(4):
                        wdn_issue(tb_ * 8)
                        wdn_issue(tb_ * 8 + 1)
                        for j in range(NJ):
                            idx = tb_ * NJ + j
                            wu = wup[idx % NWU]
                            wut = "wup%d" % (idx % NWU)
                            wup_issue(idx + NWU - 1)
                            bks = [proj(lambda k, wu=wu, gv=gv: wu[:, k, gv, :], wut, tb_) for gv in range(2)]
                            rws = [(rawf[(j * 2 + gv) % 4], "rawf%d" % ((j * 2 + gv) % 4)) for gv in range(2)]
                            accs = [(ag[j % 2], "ag%d" % (j % 2)), (av[j % 2], "av%d" % (j % 2))]
                            for gv in range(2):
                                CP("pool", rws[gv][0][:, 0:2], fhalo[:, j, gv, :], ["fhalo"], [rws[gv][1]])
                            for gv in range(2):
                                CP("act", rws[gv][0][:, 2:514], ps[bks[gv]][:], [pst[bks[gv]]], [rws[gv][1]])
                            for gv in range(2):
                                CP("pool", fhalo[:, j, gv, :], rws[gv][0][:, 512:514], [rws[gv][1]], ["fhalo"])
                            for gv in range(2):
                                ch = gv * NJ + j
                                ACT(accs[gv][0][:], rws[gv][0][:, 0:512], AF.Identity, [rws[gv][1], "pfm"], [accs[gv][1]],
                                    scale=pfm[:, ofw + ch * 3:ofw + ch * 3 + 1], bias=pfm[:, ofb + ch:ofb + ch + 1])
                            for gv in range(2):
                                ch = gv * NJ + j
                                for t_ in (1, 2):
                                    STT(accs[gv][0][:], rws[gv][0][:, t_:t_ + 512], pfm[:, ofw + ch * 3 + t_:ofw + ch * 3 + t_ + 1],
                                        accs[gv][0][:], ALU.mult, ALU.add, [rws[gv][1], accs[gv][1], "pfm"], [accs[gv][1]])
                            ACT(accs[0][0][:], accs[0][0][:], AF.Silu, [accs[0][1]], [accs[0][1]])
                            TT("dve", actT[:, j, :], accs[0][0][:], accs[1][0][:], ALU.mult, [accs[0][1], accs[1][1]], ["actT"])
                        for fc in range(8):
                            idx = tb_ * 8 + fc
                            wd = wdn[idx % 2]
                            wdt_ = "wdn%d" % (idx % 2)
                            bi = bankA()
                            for j in range(NJ):
                                MM(ps[bi][:], wd[:, j, :], actT[:, j, :], j == 0, j == NJ - 1, [wdt_, "actT"], [pst[bi]])
                            if fc < 6:
                                wdn_issue(idx + 2)
                            ff = ffs[fc % 2]
                            fft = "ffs%d" % (fc % 2)
                            ACT(ff[:], ps[bi][:], AF.Identity, [pst[bi], "modT"], [fft], scale=modT[:, 5, fc, b:b + 1])
                            b2 = 4 + (fc % 4)
                            for i in range(4):
                                TR(ps[b2][:, i * 128:(i + 1) * 128], ff[:, i * 128:(i + 1) * 128], ident_f[:], [fft, "ident_f"], [pst[b2]])
                            xv = x1[:, tb_ * 4:(tb_ + 1) * 4, fc * 128:(fc + 1) * 128]
                            TT("dve", xv, xv, ps[b2][:].rearrange("p (a c) -> p a c", a=4), ALU.add,
                               [pst[b2]] + [("x1", tb_ * 4 + i) for i in range(4)], [("x1", tb_ * 4 + i) for i in range(4)])
                        for i in range(4):
                            tt = tb_ * 4 + i
                            P.dma("sp", y_d[b, tt * 128:(tt + 1) * 128, :], x1[:, tt, :], ("x1", tt), reads=[("x1", tt)])
                else:
                    for tt in range(16):
                        P.dma("sp", y_d[b, tt * 128:(tt + 1) * 128, :], x1[:, tt, :], ("x1", tt), reads=[("x1", tt)])
    P.wait_all_dma("sp")
    P.emit()
    return nc


_NC_CACHE = {}


def _prep(inp):
    f = lambda a: np.ascontiguousarray(np.asarray(a, dtype=np.float32))
    fm = lambda v: f(np.asarray(v).reshape(-1, 128).T)
    w_in = np.asarray(inp["w_in"])[0]
    w_ada = np.asarray(inp["w_ada"])[0]
    sh = {}
    sh["w_ada"] = f(w_ada.reshape(8, 128, 6, 1024).transpose(2, 1, 0, 3))
    cw = np.asarray(inp["conv_ssm_w"])[0]
    fw = np.asarray(inp["conv_ffn_w"])[0]
    parts = {
        "b_ada": fm(np.asarray(inp["b_ada"])[0]),
        "g1": fm(np.asarray(inp["norm1_g"])[0]), "g2": fm(np.asarray(inp["norm2_g"])[0]),
        "gq": fm(np.asarray(inp["q_norm_g"])[0]), "gk": fm(np.asarray(inp["k_norm_g"])[0]),
        "cw": f(cw.reshape(4, 12, 128).transpose(2, 1, 0).reshape(128, 48)),
        "cb": fm(np.asarray(inp["conv_ssm_b"])[0]),
        "dskip": fm(np.repeat(np.asarray(inp["d_skip"])[0], 64)),
        "gssm": fm(np.asarray(inp["ssm_norm_g"])[0]), "gattn": fm(np.asarray(inp["attn_norm_g"])[0]),
        "fw": f(fw.reshape(3, 44, 128).transpose(2, 1, 0).reshape(128, 132)),
        "fb": fm(np.asarray(inp["conv_ffn_b"])[0]),
    }
    pfm = np.concatenate([parts[n] for n in _PF], axis=1)
    assert pfm.shape == (128, NPF)
    sh["pfm"] = f(pfm)
    sh["bgate"] = f(np.broadcast_to(np.asarray(inp["b_ada"])[0][2048:3072][None, :], (128, 1024)))
    sh["pbc"] = f(np.concatenate([np.broadcast_to(np.asarray(inp["dt_bias"])[0][None, :], (128, 16)),
                                  np.broadcast_to(np.asarray(inp["a_log"])[0][None, :], (128, 16))], axis=1))
    wk = w_in.reshape(8, 128, 5648)
    qkv = wk[:, :, 0:3072].reshape(8, 128, 3, 8, 128)
    sh["w_qkv"] = f(qkv.transpose(3, 1, 0, 2, 4))
    z = wk[:, :, 3072:4096].reshape(8, 128, 8, 128)
    xs = wk[:, :, 4096:5120].reshape(8, 128, 8, 128)
    sh["w_zx"] = f(np.stack([z, xs], axis=3).transpose(2, 1, 0, 3, 4))
    bc = wk[:, :, 5120:5632].reshape(8, 128, 4, 128)
    sh["w_bc"] = f(bc.transpose(1, 0, 2, 3))
    sh["w_dt"] = f(wk[:, :, 5632:5648].transpose(1, 0, 2))
    sh["w_out"] = f(np.asarray(inp["w_out"])[0].reshape(16, 128, 1024).transpose(1, 0, 2))
    wu = np.asarray(inp["w_up"])[0].reshape(8, 128, 2, NJ, 128)
    sh["w_up"] = f(wu.transpose(3, 1, 0, 2, 4))
    wd = np.asarray(inp["w_down"])[0].reshape(NJ, 128, 8, 128)
    sh["w_down"] = f(wd.transpose(2, 1, 0, 3))
    return sh


def kernel(**inp):
    x = np.asarray(inp["x"], dtype=np.float32)
    c = np.asarray(inp["c"], dtype=np.float32)
    sh = _prep(inp)
    if "nc" not in _NC_CACHE:
        _NC_CACHE["nc"] = build()
    nc = _NC_CACHE["nc"]
    in_maps = []
    for i in range(8):
        m = dict(sh)
        m["x"] = np.ascontiguousarray(x[2 * i:2 * i + 2])
        m["cT"] = np.ascontiguousarray(c[2 * i:2 * i + 2].T.reshape(8, 128, 2).transpose(1, 0, 2))
        in_maps.append(m)
    res = run_bass_kernel_spmd(nc, in_maps, core_ids=list(range(8)))
    return np.concatenate([r["y"] for r in res.results], axis=0).astype(np.float32)
```

```python
import numpy as np
import concourse.bass as bass
import concourse.mybir as mybir
from concourse.alu_op_type import AluOpType as ALU
from concourse.bass_utils import run_bass_kernel_spmd

F32 = mybir.dt.float32
BF16 = mybir.dt.bfloat16
AF = mybir.ActivationFunctionType
AX = mybir.AxisListType
ENGS = ["pe", "act", "dve", "pool", "sp"]

D = 1024
S = 2048
NSEQ = 2
NT = S // 128
DFF = 2816
NJ = DFF // 128
EPS = 1e-6
NEG = -30000.0


class Prog:
    def __init__(self, nc):
        self.nc = nc
        self.ops = {e: [] for e in ENGS}
        self.last_w = {}
        self.readers = {}
        self.dma_keys = {}
        self.waited = {e: {} for e in ENGS}
        self.sems = {e: nc.alloc_semaphore("sem_" + e) for e in ENGS}

    def _deps(self, eng, reads, writes):
        deps = []

        def add(ev, raw):
            if ev is None:
                return
            w = self.waited[eng]
            if ev[0] == "c":
                if ev[1] == eng and eng == "pe":
                    return
                k = ("c", ev[1])
                if w.get(k, -1) >= ev[2]:
                    return
                w[k] = ev[2]
                deps.append(ev)
                self.ops[ev[1]][ev[2]]["signal"] = True
            else:
                k = ("d", ev[1])
                if w.get(k, -1) >= ev[2]:
                    return
                w[k] = ev[2]
                deps.append(ev)

        for t in reads:
            add(self.last_w.get(t), True)
        for t in writes:
            add(self.last_w.get(t), False)
            rd = self.readers.get(t)
            if rd:
                for ev in rd.values():
                    add(ev, False)
        return deps

    def _commit(self, ev, reads, writes):
        for t in reads:
            self.readers.setdefault(t, {})[(ev[0], ev[1])] = ev
        for t in writes:
            self.last_w[t] = ev
            self.readers[t] = {}

    def op(self, eng, name, reads, writes, *args, **kw):
        deps = self._deps(eng, reads, writes)
        idx = len(self.ops[eng])
        self.ops[eng].append({"name": name, "args": args, "kw": kw, "deps": deps,
                              "signal": False, "dma": None})
        self._commit(("c", eng, idx), reads, writes)

    def dma(self, eng, out, in_, key, reads=(), writes=()):
        deps = self._deps(eng, reads, writes)
        if key not in self.dma_keys:
            self.dma_keys[key] = [self.nc.alloc_semaphore("dsem%d" % len(self.dma_keys)), 0]
        ent = self.dma_keys[key]
        ent[1] += 16
        self.ops[eng].append({"name": "dma_start", "args": (), "kw": dict(out=out, in_=in_),
                              "deps": deps, "signal": False, "dma": (ent[0], ent[1])})
        self._commit(("d", key, ent[1]), reads, writes)

    def barrier(self):
        evs = []
        for e in ENGS:
            if self.ops[e]:
                idx = len(self.ops[e]) - 1
                while idx >= 0 and self.ops[e][idx]["name"] in (None, "dma_start"):
                    idx -= 1
                if idx >= 0:
                    evs.append(("c", e, idx))
        devs = [("d", k, v[1]) for k, v in self.dma_keys.items()]
        for e in ENGS:
            deps = []
            w = self.waited[e]
            for ev in evs:
                if ev[1] == e:
                    continue
                k = ("c", ev[1])
                if w.get(k, -1) >= ev[2]:
                    continue
                w[k] = ev[2]
                deps.append(ev)
                self.ops[ev[1]][ev[2]]["signal"] = True
            for ev in devs:
                k = ("d", ev[1])
                if w.get(k, -1) >= ev[2]:
                    continue
                w[k] = ev[2]
                deps.append(ev)
            self.ops[e].append({"name": None, "deps": deps, "signal": False, "dma": None})

    def wait_all_dma(self, eng="sp"):
        deps = [("d", k, v[1]) for k, v in self.dma_keys.items()]
        self.ops[eng].append({"name": None, "deps": deps, "signal": False, "dma": None})

    def emit(self):
        nc = self.nc
        ranks = {}
        for e in ENGS:
            r = 0
            rk = []
            for o in self.ops[e]:
                if o["signal"]:
                    r += 1
                rk.append(r)
            ranks[e] = rk

        def run(e, handle):
            regs = {}
            if e == "pool":
                regs = {"@zero": handle.to_reg(0.0), "@neg": handle.to_reg(NEG)}
            for o in self.ops[e]:
                if regs and o.get("name") == "affine_select":
                    o["kw"]["fill"] = regs[o["kw"]["fill"]]
                for ev in o["deps"]:
                    if ev[0] == "c":
                        handle.wait_ge(self.sems[ev[1]], ranks[ev[1]][ev[2]])
                    else:
                        handle.wait_ge(self.dma_keys[ev[1]][0], ev[2])
                if o["name"] is None:
                    continue
                inst = getattr(handle, o["name"])(*o["args"], **o["kw"])
                if o["dma"] is not None:
                    inst.then_inc(o["dma"][0], 16)
                elif o["signal"]:
                    inst.then_inc(self.sems[e], 1)

        with nc.Block() as block:
            @block.tensor
            def _(eng):
                run("pe", eng)

            @block.scalar
            def _(eng):
                run("act", eng)

            @block.vector
            def _(eng):
                run("dve", eng)

            @block.gpsimd
            def _(eng):
                run("pool", eng)

            @block.sync
            def _(eng):
                run("sp", eng)


_PF = {}
_o = 0
for _n, _w in [("b_ada", 48), ("g1", 8), ("g2", 8), ("gq", 1), ("gk", 1), ("cw", 48), ("cb", 12),
               ("dskip", 8), ("gssm", 8), ("gattn", 8), ("fw", 132), ("fb", 44)]:
    _PF[_n] = (_o, _w)
    _o += _w
NPF = _o


def build(nseq=NSEQ, dbg=None, phases=("attn", "ssd", "out", "ffn")):
    from contextlib import ExitStack, contextmanager
    nc = bass.Bass("TRN2", target_bir_lowering=False, dynamic_dma_scratch_size=4096)
    P = Prog(nc)
    dram = lambda n, s, k="ExternalInput": nc.dram_tensor(n, s, F32, kind=k).ap()
    x_d = dram("x", [NSEQ, S, D])
    cT_d = dram("cT", [128, 8, NSEQ])
    wada_d = dram("w_ada", [6, 128, 8, 1024])
    pfm_d = dram("pfm", [128, NPF])
    bgate_d = dram("bgate", [128, 1024])
    pbc_d = dram("pbc", [128, 32])
    wqkv_d = dram("w_qkv", [8, 128, 8, 3, 128])
    wzx_d = dram("w_zx", [8, 128, 8, 2, 128])
    wbc_d = dram("w_bc", [128, 8, 4, 128])
    wdt_d = dram("w_dt", [128, 8, 16])
    wout_d = dram("w_out", [128, 16, 1024])
    wup_d = dram("w_up", [NJ, 128, 8, 2, 128])
    wdn_d = dram("w_down", [8, 128, NJ, 128])
    y_d = dram("y", [NSEQ, S, D], "ExternalOutput")
    scr_d = nc.dram_tensor("scr_acum", [16, S], F32).ap()
    dbg_d = dram("dbg", [128, dbg], "ExternalOutput") if dbg else None

    cnt = [0]

    @contextmanager
    def Scope():
        with ExitStack() as es:
            yield es
        P.barrier()

    def sb(n, s, dt=F32, st=None, side=None):
        cnt[0] += 1
        nm = "%s_%d" % (n, cnt[0])
        if st is None:
            return nc.alloc_sbuf_tensor(nm, s, dt)
        if side is not None:
            return st.enter_context(nc.sbuf_tensor(nm, s, dt, side=side))
        return st.enter_context(nc.sbuf_tensor(nm, s, dt))

    ps = [nc.alloc_psum_tensor("ps%d" % i, [128, 512], F32) for i in range(8)]
    psb = [p[:].bitcast(BF16) for p in ps]
    pst = ["ps%d" % i for i in range(8)]
    rot = {"A": 0}

    def bankA():
        i = rot["A"] % 4
        rot["A"] += 1
        return i

    def MM(out, lhsT, rhs, start, stop, r, w):
        P.op("pe", "matmul", r, w, out, lhsT=lhsT, rhs=rhs, start=start, stop=stop)

    def TR(out, in_, ident, r, w):
        P.op("pe", "transpose", r, w, out=out, in_=in_, identity=ident)

    def ACT(out, in_, func, r, w, **kw):
        P.op("act", "activation", r, w, out=out, in_=in_, func=func, **kw)

    def TT(eng, out, in0, in1, op, r, w):
        P.op(eng, "tensor_tensor", r, w, out=out, in0=in0, in1=in1, op=op)

    def TS(eng, out, in0, s1, s2, op0, op1, r, w):
        if op1 is None and eng == "pool":
            s2, op1 = 0.0, ALU.add
        if op1 is None:
            P.op(eng, "tensor_scalar", r, w, out=out, in0=in0, scalar1=s1, scalar2=None, op0=op0)
        else:
            P.op(eng, "tensor_scalar", r, w, out=out, in0=in0, scalar1=s1, scalar2=s2, op0=op0, op1=op1)

    def STT(out, in0, scalar, in1, op0, op1, r, w):
        P.op("dve", "scalar_tensor_tensor", r, w, out=out, in0=in0, scalar=scalar, in1=in1, op0=op0, op1=op1)

    def CP(eng, out, in_, r, w):
        if eng == "act":
            ACT(out, in_, AF.Copy, r, w)
        else:
            P.op(eng, "tensor_copy", r, w, out=out, in_=in_)

    def dump(ap, col, toks):
        if dbg_d is not None:
            n = 1
            for d_ in ap.shape[1:]:
                n *= d_
            P.dma("pool", dbg_d[0:ap.shape[0], col:col + n], ap, ("dbg", col), reads=toks)

    ident_f = sb("ident_f", [128, 128])
    ident_b = sb("ident_b", [128, 128], BF16)
    ones_f = sb("ones_f", [128, 128])
    ones_b = sb("ones_b", [128, 128], BF16)
    eblk = sb("eblk", [128, 8, 128], BF16)
    T0 = sb("T0", [128, 256])
    T1 = sb("T1", [128, 256])
    epsT = sb("epsT", [128, 1])
    oneT = sb("oneT", [128, 1])
    pfm = sb("pfm_sb", [128, NPF])
    pbc = sb("pbc_sb", [128, 32])
    a_bc = sb("a_bc", [128, 16])
    modT = sb("modT", [128, 6, 8, NSEQ])
    s1 = sb("s1", [128, 8, NSEQ])
    s2 = sb("s2", [128, 8, NSEQ])
    gate1_bc = sb("gate1_bc", [128, NSEQ, 1024])
    cT = sb("cT_sb", [128, 8, NSEQ])
    sc_b = sb("sc_b", [128, 8, NSEQ], BF16)
    gm = sb("gm", [128, 8, 8])
    top8 = sb("top8", [128, 8, 8])
    negm = sb("negm", [128, 8, 8])
    nmT = sb("nmT", [128, S], BF16)
    ssq_mix = sb("ssq_mix", [128, 3, 16])
    rstd_mix = sb("rstd_mix", [128, 3, 16])
    stat = sb("stat", [128, 3, 8])
    hT = sb("hT", [128, 8, S], BF16)

    def pf(name, i=0, n=1):
        o, w = _PF[name]
        return pfm[:, o + i:o + i + n]

    P.dma("sp", pfm[:], pfm_d, "pfm", writes=["pfm"])
    P.dma("sp", pbc[:], pbc_d, "pbc", writes=["pbc"])
    P.dma("sp", cT[:], cT_d, "cT", writes=["cT"])
    for b in range(NSEQ):
        P.dma("sp", gate1_bc[:, b, :], bgate_d, ("g1bc", b), writes=["gate1_bc"])
    with Scope() as st:
        big_ones = sb("big_ones", [128, 1024], BF16, st)
        ones256 = sb("ones256", [128, 256], F32, st)
        sc_rep = sb("sc_rep", [128, 8, NSEQ, 128], BF16, st)
        wa = [sb("wa", [128, 8, 1024], BF16, st) for _ in range(2)]
        P.op("pool", "memset", [], ["ones_f"], ones_f[:], 1.0)
        P.op("pool", "memset", [], ["ones_b"], ones_b[:], 1.0)
        P.op("pool", "memset", [], ["big_ones"], big_ones[:], 1.0)
        P.op("pool", "memset", [], ["ones256"], ones256[:], 1.0)
        P.op("pool", "memset", [], ["epsT"], epsT[:], EPS)
        P.op("pool", "memset", [], ["oneT"], oneT[:], 1.0)
        P.op("pool", "memset", [], ["nmT0"], nmT[:], 0.0)
        P.op("pool", "memset", [], ["gm"], gm[:], -1e30)
        P.op("pool", "affine_select", ["ones_f"], ["ident_f"], out=ident_f[:], in_=ones_f[:], pattern=[[-1, 128]],
             compare_op=ALU.is_equal, fill="@zero", base=0, channel_multiplier=1)
        P.op("pool", "affine_select", ["ones_b"], ["ident_b"], out=ident_b[:], in_=ones_b[:], pattern=[[-1, 128]],
             compare_op=ALU.is_equal, fill="@zero", base=0, channel_multiplier=1)
        P.op("pool", "affine_select", ["big_ones"], ["eblk"], out=eblk[:].rearrange("p a b -> p (a b)"),
             in_=big_ones[:], pattern=[[-1, 8], [0, 128]], compare_op=ALU.is_equal, fill="@zero", base=0,
             channel_multiplier=1)
        P.op("pool", "affine_select", ["ones256"], ["T0"], out=T0[:], in_=ones256[:], pattern=[[1, 256]],
             compare_op=ALU.is_ge, fill="@zero", base=0, channel_multiplier=-1)
        P.op("pool", "affine_select", ["ones256"], ["T1"], out=T1[:], in_=ones256[:], pattern=[[1, 256]],
             compare_op=ALU.is_ge, fill="@zero", base=-128, channel_multiplier=-1)
        ACT(a_bc[:], pbc[:, 16:32], AF.Exp, ["pbc"], ["a_bc"])
        TS("dve", a_bc[:], a_bc[:], -1.0, None, ALU.mult, None, ["a_bc"], ["a_bc"])
        ACT(cT[:], cT[:], AF.Silu, ["cT"], ["cT"])
        CP("dve", sc_b[:], cT[:], ["cT"], ["sc_b"])
        CP("dve", sc_rep[:], cT[:].unsqueeze(3).to_broadcast([128, 8, NSEQ, 128]), ["cT"], ["sc_rep"])
        ob, _ = _PF["b_ada"]
        for m in range(6):
            wb = wa[m % 2]
            wt = "wa%d" % (m % 2)
            P.dma("pool", wb[:], wada_d[m], wt, writes=[wt])
            if m == 2:
                for b in range(NSEQ):
                    for nch in range(2):
                        bi = bankA()
                        for k in range(8):
                            MM(ps[bi][:], sc_rep[:, k, b, :], wb[:, k, nch * 512:(nch + 1) * 512], k == 0, k == 7,
                               [wt, "sc_rep"], [pst[bi]])
                        gsl = gate1_bc[:, b, nch * 512:(nch + 1) * 512]
                        TT("dve", gsl, ps[bi][:], gsl, ALU.add, [pst[bi], "gate1_bc"], ["gate1_bc"])
            else:
                bi = bankA()
                for fc in range(8):
                    for k in range(8):
                        MM(ps[bi][:, fc * NSEQ:(fc + 1) * NSEQ], wb[:, k, fc * 128:(fc + 1) * 128], sc_b[:, k, :],
                           k == 0, k == 7, [wt, "sc_b"], [pst[bi]])
                TT("dve", modT[:, m, :, :], ps[bi][:, 0:8 * NSEQ].rearrange("p (a b) -> p a b", b=NSEQ),
                   pfm[:, ob + m * 8:ob + m * 8 + 8].unsqueeze(2).to_broadcast([128, 8, NSEQ]), ALU.add,
                   [pst[bi], "pfm"], ["modT"])
        for (sx, mi, gname) in ((s1, 1, "g1"), (s2, 4, "g2")):
            og, _ = _PF[gname]
            TS("dve", sx[:], modT[:, mi, :, :], 1.0, None, ALU.add, None, ["modT"], ["sx"])
            TT("dve", sx[:], sx[:], pfm[:, og:og + 8].unsqueeze(2).to_broadcast([128, 8, NSEQ]), ALU.mult,
               ["sx", "pfm"], ["sx"])

    nrm = {"i": 0}

    hTt = ["hT"] * 8

    def norm_to_hT(st, b, src_fn, scl, shift_m):
        xn = [sb("xn", [128, 1024], F32, st) for _ in range(4)]
        junk = [sb("junk", [128, 1024], F32, st) for _ in range(2)]
        for q4 in range(4):
            srcs = [src_fn(q4 * 4 + i) for i in range(4)]
            sl0 = (nrm["i"] % 2) * 4
            nrm["i"] += 1
            stt = ("stat", sl0)
            for i in range(4):
                ACT(junk[i % 2][:], srcs[i][0], AF.Square, [srcs[i][1]], ["junk%d" % (i % 2)])
                P.op("dve", "tensor_reduce", ["junk%d" % (i % 2)], [stt], out=stat[:, 0, sl0 + i:sl0 + i + 1], in_=junk[i % 2][:],
                     axis=AX.X, op=ALU.add)
            for i in range(4):
                ACT(stat[:, 1, sl0 + i:sl0 + i + 1], stat[:, 0, sl0 + i:sl0 + i + 1], AF.Sqrt, [stt, "epsT"], [stt], scale=1.0 / D,
                    bias=epsT[:, 0:1])
            for i in range(4):
                P.op("dve", "reciprocal", [stt], [stt], out=stat[:, 2, sl0 + i:sl0 + i + 1], in_=stat[:, 1, sl0 + i:sl0 + i + 1])
            for i in range(4):
                eng = "pool"
                TS(eng, xn[i][:], srcs[i][0], stat[:, 2, sl0 + i:sl0 + i + 1], None, ALU.mult, None, [srcs[i][1], stt], ["xn%d" % i])
            for h2 in range(2):
                hg = q4 * 2 + h2
                base = 0 if hg % 2 == 0 else 4
                for i in range(2):
                    xb_, xnt = xn[h2 * 2 + i], "xn%d" % (h2 * 2 + i)
                    for k in range(8):
                        bi = base + k // 2
                        c0 = ((k % 2) * 2 + i) * 128
                        TR(ps[bi][:, c0:c0 + 128], xb_[:, k * 128:(k + 1) * 128], ident_f[:], [xnt, "ident_f"], [pst[bi]])
                for k in range(8):
                    bi = base + k // 2
                    c0 = (k % 2) * 256
                    dst = hT[:, k, hg * 256:(hg + 1) * 256]
                    if k % 2 == 0:
                        ACT(dst, ps[bi][:, c0:c0 + 256], AF.Identity, [pst[bi], "sx", "modT"], [hTt[k]],
                            scale=scl[:, k, b:b + 1], bias=modT[:, shift_m, k, b:b + 1])
                    else:
                        TS("dve", dst, ps[bi][:, c0:c0 + 256], scl[:, k, b:b + 1], modT[:, shift_m, k, b:b + 1],
                           ALU.mult, ALU.add, [pst[bi], "sx", "modT"], [hTt[k]])

    def proj(w_ap_fn, wtok, g, extra_r=()):
        bi = bankA()
        for k in range(8):
            MM(ps[bi][:], w_ap_fn(k), hT[:, k, g * 512:(g + 1) * 512], k == 0, k == 7,
               [wtok, hTt[k]] + list(extra_r), [pst[bi]])
        return bi

    for b in range(nseq):
        st_seq = ExitStack()
        if True:
            ymixT = sb("ymixT", [128, 16, S], BF16, st_seq)
            with Scope() as st:
                xts = [sb("xt", [128, 1024], F32, st) for _ in range(4)]

                def src1(tt, xts=xts, b=b):
                    t = xts[tt % 4]
                    tok = "xt%d" % (tt % 4)
                    P.dma("sp", t[:], x_d[b, tt * 128:(tt + 1) * 128, :], tok, writes=[tok])
                    return t[:], tok
                norm_to_hT(st, b, src1, s1, 0)
            P.op("pool", "memset", [], ["ssq_mix"], ssq_mix[:], 0.0)
            if dbg and b == 0:
                dump(hT[:, 0, 0:512], 0, [hTt[0]])

            if "attn" in phases:
              with Scope() as st:
                wq = [sb("wqkv", [128, 8, 3, 128], BF16, st) for _ in range(2)]
                qf = [sb("qf", [128, 512], F32, st) for _ in range(2)]
                sq = [sb("sq", [128, 512], BF16, st) for _ in range(2)]
                lnv = sb("lnv", [128, 512], F32, st)
                rs = sb("rs", [128, 512], F32, st)
                qT = [sb("qT", [128, S], BF16, st) for _ in range(2)]
                kT = [sb("kT", [128, S], BF16, st) for _ in range(2)]
                vT = sb("vT", [128, S], BF16, st)
                vtok = [sb("vtok", [128, 16, 128], BF16, st) for _ in range(2)]
                nm2 = [nmT, sb("nmT2", [128, S], BF16, st)]
                PT = [sb("PT", [128, 512], BF16, st) for _ in range(6)]
                lnd = sb("lnd", [128, 512], F32, st)
                rd = sb("rd", [128, 512], F32, st)
                yf = sb("yf", [128, 512], F32, st)
                ysq = sb("ysq", [128, 512], BF16, st)
                kmf = sb("kmf", [128, 8], F32, st)
                kmb = sb("kmb", [128, 8], BF16, st)
                P.op("pool", "memset", [], ["nmT1"], nm2[1][:], 0.0)
                ptc = [0]
                uct = [0]

                def wq_issue(h):
                    if h < 8:
                        P.dma("pool", wq[h % 2][:], wqkv_d[h], "wqkv%d" % (h % 2), writes=["wqkv%d" % (h % 2)])

                def proj_units(h):
                    hb = h % 2
                    wtk = "wqkv%d" % hb
                    qTt, kTt, vtt, nmt = "qT%d" % hb, "kT%d" % hb, "vtok%d" % hb, "nmT%d" % hb
                    units = []
                    for ti, (dstT, dtok, gn) in enumerate(((qT[hb], qTt, "gq"), (kT[hb], kTt, "gk"))):
                        for g in range(4):
                            sl = (ti * 4 + g) % 2

                            def u1(ti=ti, g=g, sl=sl):
                                bi = proj(lambda k: wq[hb][:, k, ti, :], wtk, g)
                                CP("dve", qf[sl][:], ps[bi][:], [pst[bi]], ["qf%d" % sl])
                                TT("pool", sq[sl][:], qf[sl][:], qf[sl][:], ALU.mult, ["qf%d" % sl], ["sq%d" % sl])

                            def u2(g=g, sl=sl, dstT=dstT, dtok=dtok, gn=gn):
                                b2 = bankA()
                                MM(ps[b2][:], ones_b[:], sq[sl][:], True, True, ["ones_b", "sq%d" % sl], [pst[b2]])
                                ACT(lnv[:], ps[b2][:], AF.Ln, [pst[b2], "epsT"], ["lnv"], scale=1.0 / 128, bias=epsT[:, 0:1])
                                ACT(rs[:], lnv[:], AF.Exp, ["lnv"], ["rs"], scale=-0.5)
                                STT(dstT[:, g * 512:(g + 1) * 512], qf[sl][:], pf(gn), rs[:], ALU.mult, ALU.mult,
                                    ["qf%d" % sl, "rs", "pfm"], [dtok])
                            units += [u1, u2]
                    u1s, u2s = units[0::2], units[1::2]
                    units = [u1s[0]]
                    for k_ in range(1, 8):
                        units += [u1s[k_], u2s[k_ - 1]]
                    units.append(u2s[7])
                    for g in range(4):
                        def uv1(g=g):
                            bi = proj(lambda k: wq[hb][:, k, 2, :], wtk, g)
                            CP("act", vT[:, g * 512:(g + 1) * 512], ps[bi][:], [pst[bi]], ["vT"])

                        def uv2(g=g):
                            bi = bankA()
                            for i in range(4):
                                tt = g * 4 + i
                                TR(psb[bi][:, i * 128:(i + 1) * 128], vT[:, tt * 128:(tt + 1) * 128], ident_b[:],
                                   ["vT", "ident_b"], [pst[bi]])
                            CP("dve", vtok[hb][:, g * 4:(g + 1) * 4, :].rearrange("p a b -> p (a b)"), psb[bi][:, 0:512],
                               [pst[bi]], [vtt])
                        units += [uv1, uv2]

                    def ug1():
                        P.op("dve", "tensor_reduce", [kTt], ["kmf"], out=kmf[:], in_=kT[hb][:].rearrange("p (n t) -> p n t", t=256),
                             axis=AX.X, op=ALU.add)
                        CP("dve", kmb[:], kmf[:], ["kmf"], ["kmb"])

                    gst = {}

                    def ug2():
                        bi = bankA()
                        gst["bi"] = bi
                        for i in range(8):
                            tt = 8 + i
                            MM(ps[bi][:, i * 8:(i + 1) * 8], qT[hb][:, tt * 128:(tt + 1) * 128], kmb[:], True, True,
                               [qTt, "kmb"], [pst[bi]])
                        for i in range(8):
                            own = (8 + i) // 2
                            CP("dve", gm[:, i, 0:own], ps[bi][:, i * 8:i * 8 + own], [pst[bi], "gm"], ["gm"])
                        for i in range(8):
                            P.op("dve", "max", ["gm"], ["top8"], out=top8[:, i, :], in_=gm[:, i, :])
                        TT("dve", negm[:], gm[:], top8[:, :, 2:3].to_broadcast([128, 8, 8]), ALU.is_lt, ["gm", "top8"], ["negm"])
                        TS("dve", negm[:], negm[:], NEG, None, ALU.mult, None, ["negm"], ["negm"])

                    def ug3():
                        for g2 in range(2):
                            bi = bankA()
                            for i in range(4):
                                TR(ps[bi][0:8, i * 128:(i + 1) * 128], negm[:, g2 * 4 + i, :], ident_f[:], ["negm", "ident_f"],
                                   [pst[bi]])
                            CP("dve", nm2[hb][0:8, 1024 + g2 * 512:1024 + (g2 + 1) * 512], ps[bi][0:8, :], [pst[bi]], [nmt])
                    units += [ug1, ug2, ug3]
                    return units

                def main_head(h, side):
                    hb = h % 2
                    qTt, kTt, vtt, nmt = "qT%d" % hb, "kT%d" % hb, "vtok%d" % hb, "nmT%d" % hb

                    def stage1(j, kt):
                        blk = kt // 2
                        A, Bk = 2 * j, 2 * j + 1
                        c0 = 256 if blk == Bk else 0
                        bi = bankA()
                        need_mask = (j >= 2) and (blk < Bk)
                        MM(ps[bi][:, c0:512], kT[hb][:, kt * 128:(kt + 1) * 128], qT[hb][:, j * 512 + c0:(j + 1) * 512],
                           True, not need_mask, [kTt, qTt], [pst[bi]])
                        if need_mask:
                            m0 = 256 if blk == A else 0
                            MM(ps[bi][:, m0:512], eblk[:, blk, :], nm2[hb][:, j * 512 + m0:(j + 1) * 512], False, True,
                               ["eblk", nmt], [pst[bi]])
                        pt = PT[ptc[0] % 6]
                        ptk = "PT%d" % (ptc[0] % 6)
                        ptc[0] += 1
                        ACT(pt[:, c0:512], ps[bi][:, c0:512], AF.Exp, [pst[bi]], [ptk], scale=128 ** -0.5)
                        if blk >= A:
                            r = kt % 2
                            P.op("pool", "affine_select", [ptk], [ptk], out=pt[:, c0:c0 + 256], in_=pt[:, c0:c0 + 256],
                                 pattern=[[1, 256]], compare_op=ALU.is_ge, fill="@zero", base=-128 * r,
                                 channel_multiplier=-1)
                        return (j, kt, c0, pt, ptk)

                    def stage2(ctx):
                        j, kt, c0, pt, ptk = ctx
                        bo, bd = 4 + (j % 2) * 2, 5 + (j % 2) * 2
                        nk = 4 * (j + 1)
                        MM(ps[bo][:, c0:512], vtok[hb][:, kt, :], pt[:, c0:512], kt == 0, kt == nk - 1,
                           [vtt, ptk], [pst[bo]])
                        MM(ps[bd][:, c0:512], ones_b[:], pt[:, c0:512], kt == 0, kt == nk - 1,
                           ["ones_b", ptk], [pst[bd]])
                        if kt == nk - 1:
                            ACT(lnd[:], ps[bd][:], AF.Ln, [pst[bd]], ["lnd"])
                            ACT(rd[:], lnd[:], AF.Exp, ["lnd"], ["rd"], scale=-1.0)
                            TT("dve", yf[:], ps[bo][:], rd[:], ALU.mult, [pst[bo], "rd"], ["yf"])
                            TT("pool", ysq[:], yf[:], yf[:], ALU.mult, ["yf"], ["ysq"])
                            TS("pool", ymixT[:, h, j * 512:(j + 1) * 512], yf[:], pf("gattn", h), None, ALU.mult, None,
                               ["yf", "pfm"], ["ymixT"])
                            def ssq_fn(j=j):
                                bi = bankA()
                                for i in range(4):
                                    MM(ps[bi][:, i:i + 1], ysq[:, i * 128:(i + 1) * 128], ones_b[:, 0:1], True, True,
                                       ["ysq", "ones_b"], [pst[bi]])
                                TT("dve", ssq_mix[:, 0, j * 4:(j + 1) * 4], ssq_mix[:, 0, j * 4:(j + 1) * 4], ps[bi][:, 0:4],
                                   ALU.add, [pst[bi], "ssq_mix"], ["ssq_mix"])
                            defer.append([4, ssq_fn])

                    tiles = [(j, kt) for j in range(4) for kt in range(4 * (j + 1))]
                    pend = []
                    defer = []
                    for (j, kt) in tiles:
                        pend.append(stage1(j, kt))
                        if len(pend) > 3:
                            stage2(pend.pop(0))
                        for d_ in list(defer):
                            d_[0] -= 1
                            if d_[0] <= 0:
                                defer.remove(d_)
                                d_[1]()
                        if side:
                            side.pop(0)()
                    while pend:
                        stage2(pend.pop(0))
                    while side:
                        side.pop(0)()
                    for d_ in defer:
                        d_[1]()

                wq_issue(0)
                wq_issue(1)
                for u in proj_units(0):
                    u()
                for h in range(8):
                    side = proj_units(h + 1) if h + 1 < 8 else []
                    main_head(h, side)
                    wq_issue(h + 2)
            if dbg and b == 0:
                dump(ymixT[:, 0, 0:512], 512, ["ymixT"])
                dump(ymixT[:, 7, 1536:2048], 1024, ["ymixT"])

            if "ssd" in phases:
              with Scope() as st:
                wdt = sb("wdt", [128, 8, 16], BF16, st)
                wbc = sb("wbc", [128, 8, 4, 128], BF16, st)
                wzx = [sb("wzx", [128, 8, 2, 128], BF16, st) for _ in range(3)]
                dtr = sb("dtr", [128, 16, 16], F32, st)
                dtl = sb("dtl", [128, 16, 16], F32, st)
                dt_tok = sb("dt_tok", [128, 16, 16], F32, st)
                da_tok = sb("da_tok", [128, 16, 16], F32, st)
                nacum = sb("nacum", [128, 16, 16], F32, st)
                tot = sb("tot", [128, 8, 16], F32, st)
                cdec = sb("cdec", [128, 8, 16], F32, st)
                dtx = sb("dtx", [128, 16, 16], F32, st)
                acs = sb("acs", [16, 512], F32, st)
                raw = [sb("raw", [128, 515], F32, st) for _ in range(2)]
                halo = sb("halo", [128, 12, 3], F32, st)
                acc = [sb("acc", [128, 512], F32, st) for _ in range(3)]
                szT = [sb("szT", [128, 512], F32, st) for _ in range(3)]
                BT = sb("BT", [128, 2, 512], BF16, st)
                CTt = sb("CT", [128, 2, 512], BF16, st)
                Btok = sb("Btok", [128, 2, 4, 128], BF16, st)
                cb = sb("cb", [128, 2, 2, 384], BF16, st)
                xdt = [sb("xdt", [128, 4, 128], BF16, st) for _ in range(2)]
                xdtt = [sb("xdtt", [128, 4, 128], BF16, st) for _ in range(2)]
                bcs = [sb("bc", [128, 512], F32, st) for _ in range(6)]
                tmp = [sb("tmp", [128, 384], F32, st) for _ in range(4)]
                dec = [sb("dec", [128, 384], BF16, st) for _ in range(4)]
                MT = [sb("MT", [128, 384], BF16, st) for _ in range(4)]
                ebc = [sb("ebc", [128, 256], BF16, st) for _ in range(4)]
                Ct = [sb("Ct", [128, 256], BF16, st) for _ in range(4)]
                hin = sb("hin", [128, 16, 64], F32, st)
                hinb = [sb("hinb", [128, 2, 2, 64], BF16, st) for _ in range(2)]
                yv = sb("yv", [128, 512], F32, st)
                yg = sb("yg", [128, 512], F32, st)
                ysq2 = sb("ysq2", [128, 512], BF16, st)
                P.dma("pool", wdt[:], wdt_d, "wdt", writes=["wdt"])
                P.dma("pool", wbc[:], wbc_d, "wbc", writes=["wbc"])
                P.op("pool", "memset", [], ["hin"], hin[:], 0.0)
                P.op("pool", "memset", [], ["halo"], halo[:], 0.0)
                bi = bankA()
                for tt in range(16):
                    for k in range(8):
                        MM(ps[bi][:, tt * 16:(tt + 1) * 16], hT[:, k, tt * 128:(tt + 1) * 128], wdt[:, k, :], k == 0, k == 7,
                           [hTt[k], "wdt"], [pst[bi]])
                TT("dve", dtr[:], ps[bi][:, 0:256].rearrange("p (a b) -> p a b", b=16),
                   pbc[:, 0:16].unsqueeze(1).to_broadcast([128, 16, 16]), ALU.add, [pst[bi], "pbc"], ["dtr"])
                STT(dtl[:], dtr[:], -1.0, dtr[:], ALU.mult, ALU.max, ["dtr"], ["dtl"])
                ACT(dtl[:], dtl[:], AF.Exp, ["dtl"], ["dtl"], scale=-1.0)
                ACT(dtl[:], dtl[:], AF.Ln, ["dtl", "oneT"], ["dtl"], bias=oneT[:, 0:1])
                STT(dt_tok[:], dtr[:], 0.0, dtl[:], ALU.max, ALU.add, ["dtr", "dtl"], ["dt_tok"])
                TT("dve", da_tok[:], dt_tok[:], a_bc[:].unsqueeze(1).to_broadcast([128, 16, 16]), ALU.mult,
                   ["dt_tok", "a_bc"], ["da_tok"])
                b1, b2, b3 = bankA(), bankA(), bankA()
                for c in range(8):
                    t0_, t1_ = 2 * c, 2 * c + 1
                    MM(ps[b1][:, t0_ * 16:(t0_ + 1) * 16], T0[:, 0:128], da_tok[:, t0_, :], True, True, ["T0", "da_tok"], [pst[b1]])
                    MM(ps[b1][:, t1_ * 16:(t1_ + 1) * 16], T0[:, 128:256], da_tok[:, t0_, :], True, False, ["T0", "da_tok"], [pst[b1]])
                    MM(ps[b1][:, t1_ * 16:(t1_ + 1) * 16], T1[:, 128:256], da_tok[:, t1_, :], False, True, ["T1", "da_tok"], [pst[b1]])
                    MM(ps[b2][:, c * 16:(c + 1) * 16], ones_f[:], da_tok[:, t0_, :], True, False, ["ones_f", "da_tok"], [pst[b2]])
                    MM(ps[b2][:, c * 16:(c + 1) * 16], ones_f[:], da_tok[:, t1_, :], False, True, ["ones_f", "da_tok"], [pst[b2]])
                TS("dve", nacum[:].rearrange("p a b -> p (a b)"), ps[b1][:, 0:256], -1.0, None, ALU.mult, None, [pst[b1]], ["nacum"])
                CP("dve", tot[:].rearrange("p a b -> p (a b)"), ps[b2][:, 0:128], [pst[b2]], ["tot"])
                ACT(cdec[:], tot[:], AF.Exp, ["tot"], ["cdec"])
                TT("dve", dtx[:].rearrange("p (c t) h -> p c t h", t=2), nacum[:].rearrange("p (c t) h -> p c t h", t=2),
                   tot[:].unsqueeze(2).to_broadcast([128, 8, 2, 16]), ALU.add, ["nacum", "tot"], ["dtx"])
                ACT(dtx[:], dtx[:], AF.Exp, ["dtx"], ["dtx"])
                TT("dve", dtx[:], dtx[:], dt_tok[:], ALU.mult, ["dtx", "dt_tok"], ["dtx"])
                for g in range(4):
                    for cc in range(2):
                        c = g * 2 + cc
                        MM(ps[b3][0:16, cc * 256:(cc + 1) * 256], da_tok[:, 2 * c, :], T0[:], True, False, ["T0", "da_tok"], [pst[b3]])
                        MM(ps[b3][0:16, cc * 256:(cc + 1) * 256], da_tok[:, 2 * c + 1, :], T1[:], False, True, ["T1", "da_tok"], [pst[b3]])
                    CP("dve", acs[:], ps[b3][0:16, :], [pst[b3]], ["acs"])
                    P.dma("sp", scr_d[:, g * 512:(g + 1) * 512], acs[:], "acs", reads=["acs"], writes=[("scr", g)])

                def conv_silu(bi, cc, rw, rwt, out_ap, wtoks, g, a_, at):
                    oc, _ = _PF["cw"]
                    obb, _ = _PF["cb"]
                    CP("pool", rw[:, 0:3], halo[:, cc, :], ["halo"], [rwt])
                    CP("act", rw[:, 3:515], ps[bi][:], [pst[bi]], [rwt])
                    CP("pool", halo[:, cc, :], rw[:, 512:515], [rwt], ["halo"])
                    ACT(a_[:], rw[:, 0:512], AF.Identity, [rwt, "pfm"], [at], scale=pfm[:, oc + cc * 4:oc + cc * 4 + 1],
                        bias=pfm[:, obb + cc:obb + cc + 1])
                    for j_ in range(1, 4):
                        STT(a_[:], rw[:, j_:j_ + 512], pfm[:, oc + cc * 4 + j_:oc + cc * 4 + j_ + 1], a_[:], ALU.mult, ALU.add,
                            [rwt, at, "pfm"], [at])
                    ACT(out_ap, a_[:], AF.Silu, [at], wtoks)

                def wzx_issue(idx):
                    if idx < 32:
                        wt_ = "wzx%d" % (idx % 3)
                        P.dma("pool", wzx[idx % 3][:], wzx_d[idx % 8], wt_, writes=[wt_])

                def BCgroup(g):
                    for idx in range(4):
                        cc = 8 + idx
                        bi = proj(lambda k, idx=idx: wbc[:, k, idx, :], "wbc", g)
                        dst = BT[:, idx, :] if idx < 2 else CTt[:, idx - 2, :]
                        conv_silu(bi, cc, raw[idx % 2], "raw%d" % (idx % 2), dst, ["BT"] if idx < 2 else ["CT"], g,
                                  (yv, yg)[idx % 2], ("yv", "yg")[idx % 2])
                    for gi in range(2):
                        bi = bankA()
                        for i in range(4):
                            TR(psb[bi][:, i * 128:(i + 1) * 128], BT[:, gi, i * 128:(i + 1) * 128], ident_b[:], ["BT", "ident_b"], [pst[bi]])
                        CP("dve", Btok[:, gi, :, :].rearrange("p a b -> p (a b)"), psb[bi][:, 0:512], [pst[bi]], ["Btok"])
                        for cc in range(2):
                            bi = bankA()
                            MM(ps[bi][:, 0:256], BT[:, gi, cc * 256:cc * 256 + 128], CTt[:, gi, cc * 256:(cc + 1) * 256], True, True,
                               ["BT", "CT"], [pst[bi]])
                            MM(ps[bi][:, 256:384], BT[:, gi, cc * 256 + 128:cc * 256 + 256], CTt[:, gi, cc * 256 + 128:(cc + 1) * 256],
                               True, True, ["BT", "CT"], [pst[bi]])
                            CP("act", cb[:, gi, cc, :], ps[bi][:, 0:384], [pst[bi]], ["cb"])

                def stA1(g, hp):
                    idx = g * 8 + hp
                    wb_ = wzx[idx % 3]
                    wt_ = "wzx%d" % (idx % 3)
                    wzx_issue(idx + 2)
                    p3 = hp % 3
                    for hl in range(2):
                        hh = 2 * hp + hl
                        bk = (2 * hp + hl) % 6
                        P.dma("sp", bcs[bk][:], scr_d[hh:hh + 1, g * 512:(g + 1) * 512].partition_broadcast(128), "bc%d" % bk,
                              reads=[("scr", g)], writes=["bc%d" % bk])
                    bi = proj(lambda k: wb_[:, k, 0, :], wt_, g)
                    ACT(szT[p3][:], ps[bi][:], AF.Silu, [pst[bi]], ["szT%d" % p3])
                    bi = proj(lambda k: wb_[:, k, 1, :], wt_, g)
                    xs, xst = acc[p3], "acc%d" % p3
                    conv_silu(bi, hp, raw[hp % 2], "raw%d" % (hp % 2), xs[:], [xst], g, xs, xst)

                def stA2(g, hp):
                    pp = hp % 2
                    xs, xst = acc[hp % 3], "acc%d" % (hp % 3)
                    bi = bankA()
                    for i in range(4):
                        TR(ps[bi][:, i * 128:(i + 1) * 128], xs[:, i * 128:(i + 1) * 128], ident_f[:], [xst, "ident_f"], [pst[bi]])
                    pv = ps[bi][:].rearrange("p (a h d) -> p a h d", a=4, h=2)
                    TT("dve", xdt[pp][:].rearrange("p a (h d) -> p a h d", h=2), pv,
                       dt_tok[:, g * 4:(g + 1) * 4, 2 * hp:2 * hp + 2].unsqueeze(3).to_broadcast([128, 4, 2, 64]), ALU.mult,
                       [pst[bi], "dt_tok"], ["xdt%d" % pp])
                    TT("dve", xdtt[pp][:].rearrange("p a (h d) -> p a h d", h=2), pv,
                       dtx[:, g * 4:(g + 1) * 4, 2 * hp:2 * hp + 2].unsqueeze(3).to_broadcast([128, 4, 2, 64]), ALU.mult,
                       [pst[bi], "dtx"], ["xdtt%d" % pp])

                def _slots(hp):
                    out = []
                    for cc in range(2):
                        for hl in range(2):
                            bk = (2 * hp + hl) % 6
                            out.append((cc, hl, 2 * hp + hl, cc * 2 + hl, bcs[bk], "bc%d" % bk))
                    return out

                def stE1(g, hp):
                    slots = _slots(hp)
                    for (cc, hl, hh, sl, bcur, bcurt) in slots:
                        TS("dve", tmp[sl][:, 0:256], bcur[:, cc * 256:(cc + 1) * 256], nacum[:, g * 4 + cc * 2, hh:hh + 1], None, ALU.add, None,
                           [bcurt, "nacum"], ["tmp%d" % sl])
                        TS("dve", tmp[sl][:, 256:384], bcur[:, cc * 256 + 128:(cc + 1) * 256], nacum[:, g * 4 + cc * 2 + 1, hh:hh + 1], None,
                           ALU.add, None, [bcurt, "nacum"], ["tmp%d" % sl])
                    for (cc, hl, hh, sl, bcur, bcurt) in slots:
                        ACT(ebc[sl][:], bcur[:, cc * 256:(cc + 1) * 256], AF.Exp, [bcurt], ["ebc%d" % sl])
                    for (cc, hl, hh, sl, bcur, bcurt) in slots:
                        for q_ in (0, 256):
                            P.op("pool", "affine_select", ["tmp%d" % sl], ["tmp%d" % sl], out=tmp[sl][:, q_:q_ + 128], in_=tmp[sl][:, q_:q_ + 128],
                                 pattern=[[1, 128]], compare_op=ALU.is_ge, fill="@neg", base=0, channel_multiplier=-1)
                    for (cc, hl, hh, sl, bcur, bcurt) in slots:
                        ACT(dec[sl][:], tmp[sl][:], AF.Exp, ["tmp%d" % sl], ["dec%d" % sl])

                def stE2(g, hp):
                    gi = hp // 4
                    slots = _slots(hp)
                    for (cc, hl, hh, sl, bcur, bcurt) in slots:
                        TT("pool", Ct[sl][:], CTt[:, gi, cc * 256:(cc + 1) * 256], ebc[sl][:], ALU.mult, ["CT", "ebc%d" % sl], ["Ct%d" % sl])
                    for (cc, hl, hh, sl, bcur, bcurt) in slots:
                        TT("dve", MT[sl][:], cb[:, gi, cc, :], dec[sl][:], ALU.mult, ["cb", "dec%d" % sl], ["MT%d" % sl])

                def stB1(g, hp):
                    gi = hp // 4
                    pp = hp % 2
                    by = 4 + pp
                    bs = 6 + pp
                    hb_, hbt = hinb[pp], "hinb%d" % pp
                    for cc in range(2):
                        c = g * 2 + cc
                        for li in range(2):
                            MM(ps[bs][:, cc * 128:(cc + 1) * 128], Btok[:, gi, cc * 2 + li, :], xdtt[pp][:, cc * 2 + li, :], li == 0, li == 1,
                               ["Btok", "xdtt%d" % pp], [pst[bs]])
                        CP("dve", hb_[:, cc, :, :], hin[:, 2 * hp:2 * hp + 2, :], ["hin"], [hbt])
                        for hl in range(2):
                            hh = 2 * hp + hl
                            STT(hin[:, hh, :], hin[:, hh, :], cdec[:, c, hh:hh + 1], ps[bs][:, cc * 128 + hl * 64:cc * 128 + (hl + 1) * 64],
                                ALU.mult, ALU.add, ["hin", "cdec", pst[bs]], ["hin"])

                def stB2(g, hp):
                    pp = hp % 2
                    by = 4 + pp
                    hb_, hbt = hinb[pp], "hinb%d" % pp
                    for cc in range(2):
                        for hl in range(2):
                            sl = cc * 2 + hl
                            yo = ps[by][hl * 64:(hl + 1) * 64, cc * 256:(cc + 1) * 256]
                            MM(yo, hb_[:, cc, hl, :], Ct[sl][:], True, False, [hbt, "Ct%d" % sl], [pst[by]])
                            MM(yo, xdt[pp][:, cc * 2, hl * 64:(hl + 1) * 64], MT[sl][:, 0:256], False, False, ["xdt%d" % pp, "MT%d" % sl], [pst[by]])
                            MM(ps[by][hl * 64:(hl + 1) * 64, cc * 256 + 128:(cc + 1) * 256], xdt[pp][:, cc * 2 + 1, hl * 64:(hl + 1) * 64],
                               MT[sl][:, 256:384], False, True, ["xdt%d" % pp, "MT%d" % sl], [pst[by]])

                def stEP(g, hp):
                    gi = hp // 4
                    pp = hp % 2
                    p3 = hp % 3
                    by = 4 + pp
                    xs, xst = acc[p3], "acc%d" % p3
                    STT(yv[:], xs[:], pf("dskip", hp), ps[by][:], ALU.mult, ALU.add, [xst, "pfm", pst[by]], ["yv"])
                    TT("pool", yg[:], yv[:], szT[p3][:], ALU.mult, ["yv", "szT%d" % p3], ["yg"])
                    TT("pool", ysq2[:], yg[:], yg[:], ALU.mult, ["yg"], ["ysq2"])
                    TS("pool", ymixT[:, 8 + hp, g * 512:(g + 1) * 512], yg[:], pf("gssm", hp), None, ALU.mult, None,
                       ["yg", "pfm"], ["ymixT"])

                    def ssq_fn():
                        bi = bankA()
                        for i in range(4):
                            MM(ps[bi][:, i:i + 1], ysq2[:, i * 128:(i + 1) * 128], ones_b[:, 0:1], True, True, ["ysq2", "ones_b"], [pst[bi]])
                        TT("dve", ssq_mix[:, 1 + gi, g * 4:(g + 1) * 4], ssq_mix[:, 1 + gi, g * 4:(g + 1) * 4], ps[bi][:, 0:4],
                           ALU.add, [pst[bi], "ssq_mix"], ["ssq_mix"])
                    return ssq_fn

                wzx_issue(0)
                wzx_issue(1)
                for g in range(4):
                    BCgroup(g)
                    for k_ in range(3):
                        stA1(g, k_)
                    stA2(g, 0)
                    stA2(g, 1)
                    stE1(g, 0)
                    stE2(g, 0)
                    stE1(g, 1)
                    late = None
                    for hp in range(8):
                        stB1(g, hp)
                        stB2(g, hp)
                        if late is not None:
                            late()
                        late = stEP(g, hp)
                        if hp + 1 < 8:
                            stE2(g, hp + 1)
                        if hp + 2 < 8:
                            stA2(g, hp + 2)
                            stE1(g, hp + 2)
                        if hp + 3 < 8:
                            stA1(g, hp + 3)
                    late()
            if dbg and b == 0:
                dump(ymixT[:, 8, 0:512], 1536, ["ymixT"])
                dump(ymixT[:, 15, 1536:2048], 2048, ["ymixT"])
                if dbg > 4096:
                    for hp_ in range(8):
                        dump(ymixT[:, 8 + hp_, :], 4096 + hp_ * 2048, ["ymixT"])

            with Scope() as st3:
                x1 = sb("x1", [128, 16, 1024], F32, st3, side="right")
                for r_ in range(3):
                    ACT(rstd_mix[:, r_, :], ssq_mix[:, r_, :], AF.Sqrt, ["ssq_mix", "epsT"], ["rstd_mix"],
                        scale=1.0 / (1024 if r_ == 0 else 512), bias=epsT[:, 0:1])
                P.op("dve", "reciprocal", ["rstd_mix"], ["rstd_mix"], out=rstd_mix[:].rearrange("p a b -> p (a b)"),
                     in_=rstd_mix[:].rearrange("p a b -> p (a b)"))
                with Scope() as st:
                    wout = sb("wout", [128, 16, 1024], BF16, st)
                    tbuf = [sb("tbuf", [128, 512], F32, st) for _ in range(2)]
                    for k4 in range(4):
                        P.dma("pool", wout[:, k4 * 4:(k4 + 1) * 4, :], wout_d[:, k4 * 4:(k4 + 1) * 4, :], ("wout", k4), writes=["wout"])
                    for tt in range(16):
                        xtok = ("x1", tt)
                        P.dma("sp", x1[:, tt, :], x_d[b, tt * 128:(tt + 1) * 128, :], xtok, writes=[xtok])
                        for nch in range(2):
                            cs = slice(nch * 512, (nch + 1) * 512)
                            bks = []
                            for (k0, k1) in ((0, 8), (8, 12), (12, 16)):
                                bi = bankA()
                                bks.append(bi)
                                for k in range(k0, k1):
                                    MM(ps[bi][:], ymixT[:, k, tt * 128:(tt + 1) * 128], wout[:, k, cs], k == k0, k == k1 - 1,
                                       ["ymixT", "wout"], [pst[bi]])
                            tb = tbuf[(tt * 2 + nch) % 2]
                            tbt = "tbuf%d" % ((tt * 2 + nch) % 2)
                            ACT(tb[:], ps[bks[0]][:], AF.Identity, [pst[bks[0]], "rstd_mix"], [tbt], scale=rstd_mix[:, 0, tt:tt + 1])
                            STT(tb[:], ps[bks[1]][:], rstd_mix[:, 1, tt:tt + 1], tb[:], ALU.mult, ALU.add, [pst[bks[1]], "rstd_mix", tbt], [tbt])
                            STT(tb[:], ps[bks[2]][:], rstd_mix[:, 2, tt:tt + 1], tb[:], ALU.mult, ALU.add, [pst[bks[2]], "rstd_mix", tbt], [tbt])
                            TT("pool", tb[:], tb[:], gate1_bc[:, b, cs], ALU.mult, [tbt, "gate1_bc"], [tbt])
                            TT("pool", x1[:, tt, cs], x1[:, tt, cs], tb[:], ALU.add, [tbt, xtok], [xtok])
                if dbg and b == 0:
                    dump(x1[:, 0, 0:512], 2560, [("x1", 0)])
                st_seq.close()
                P.barrier()
                with Scope() as st:
                    def src2(tt, x1=x1):
                        return x1[:, tt, :], ("x1", tt)
                    norm_to_hT(st, b, src2, s2, 3)
                if "ffn" in phases:
                  with Scope() as st:
                    NWU = 5
                    wup = [sb("wup", [128, 8, 2, 128], BF16, st) for _ in range(NWU)]
                    wdn = [sb("wdn", [128, NJ, 128], BF16, st) for _ in range(2)]
                    actT = sb("actT", [128, NJ, 512], BF16, st)
                    rawf = [sb("rawf", [128, 514], F32, st) for _ in range(4)]
                    fhalo = sb("fhalo", [128, NJ, 2, 2], F32, st)
                    ag = [sb("ag", [128, 512], F32, st) for _ in range(2)]
                    av = [sb("av", [128, 512], F32, st) for _ in range(2)]
                    ffs = [sb("ffs", [128, 512], F32, st) for _ in range(2)]
                    P.op("pool", "memset", [], ["fhalo"], fhalo[:], 0.0)
                    ofw, _ = _PF["fw"]
                    ofb, _ = _PF["fb"]
                    def wup_issue(idx):
                        if idx < 4 * NJ:
                            wt_ = "wup%d" % (idx % NWU)
                            P.dma("pool", wup[idx % NWU][:], wup_d[idx % NJ], wt_, writes=[wt_])

                    def wdn_issue(idx):
                        if idx < 4 * 8:
                            wt_ = "wdn%d" % (idx % 2)
                            P.dma("pool", wdn[idx % 2][:], wdn_d[idx % 8], wt_, writes=[wt_])
                    for i_ in range(NWU - 1):
                        wup_issue(i_)
                    for tb_ in range(4):
                        wdn_issue(tb_ * 8)
                        wdn_issue(tb_ * 8 + 1)
                        for j in range(NJ):
                            idx = tb_ * NJ + j
                            wu = wup[idx % NWU]
                            wut = "wup%d" % (idx % NWU)
                            wup_issue(idx + NWU - 1)
                            bks = [proj(lambda k, wu=wu, gv=gv: wu[:, k, gv, :], wut, tb_) for gv in range(2)]
                            rws = [(rawf[(j * 2 + gv) % 4], "rawf%d" % ((j * 2 + gv) % 4)) for gv in range(2)]
                            accs = [(ag[j % 2], "ag%d" % (j % 2)), (av[j % 2], "av%d" % (j % 2))]
                            for gv in range(2):
                                CP("pool", rws[gv][0][:, 0:2], fhalo[:, j, gv, :], ["fhalo"], [rws[gv][1]])
                            for gv in range(2):
                                CP("act", rws[gv][0][:, 2:514], ps[bks[gv]][:], [pst[bks[gv]]], [rws[gv][1]])
                            for gv in range(2):
                                CP("pool", fhalo[:, j, gv, :], rws[gv][0][:, 512:514], [rws[gv][1]], ["fhalo"])
                            for gv in range(2):
                                ch = gv * NJ + j
                                ACT(accs[gv][0][:], rws[gv][0][:, 0:512], AF.Identity, [rws[gv][1], "pfm"], [accs[gv][1]],
                                    scale=pfm[:, ofw + ch * 3:ofw + ch * 3 + 1], bias=pfm[:, ofb + ch:ofb + ch + 1])
                            for gv in range(2):
                                ch = gv * NJ + j
                                for t_ in (1, 2):
                                    STT(accs[gv][0][:], rws[gv][0][:, t_:t_ + 512], pfm[:, ofw + ch * 3 + t_:ofw + ch * 3 + t_ + 1],
                                        accs[gv][0][:], ALU.mult, ALU.add, [rws[gv][1], accs[gv][1], "pfm"], [accs[gv][1]])
                            ACT(accs[0][0][:], accs[0][0][:], AF.Silu, [accs[0][1]], [accs[0][1]])
                            TT("dve", actT[:, j, :], accs[0][0][:], accs[1][0][:], ALU.mult, [accs[0][1], accs[1][1]], ["actT"])
                        for fc in range(8):
                            idx = tb_ * 8 + fc
                            wd = wdn[idx % 2]
                            wdt_ = "wdn%d" % (idx % 2)
                            bi = bankA()
                            for j in range(NJ):
                                MM(ps[bi][:], wd[:, j, :], actT[:, j, :], j == 0, j == NJ - 1, [wdt_, "actT"], [pst[bi]])
                            if fc < 6:
                                wdn_issue(idx + 2)
                            ff = ffs[fc % 2]
                            fft = "ffs%d" % (fc % 2)
                            ACT(ff[:], ps[bi][:], AF.Identity, [pst[bi], "modT"], [fft], scale=modT[:, 5, fc, b:b + 1])
                            b2 = 4 + (fc % 4)
                            for i in range(4):
                                TR(ps[b2][:, i * 128:(i + 1) * 128], ff[:, i * 128:(i + 1) * 128], ident_f[:], [fft, "ident_f"], [pst[b2]])
                            xv = x1[:, tb_ * 4:(tb_ + 1) * 4, fc * 128:(fc + 1) * 128]
                            TT("dve", xv, xv, ps[b2][:].rearrange("p (a c) -> p a c", a=4), ALU.add,
                               [pst[b2]] + [("x1", tb_ * 4 + i) for i in range(4)], [("x1", tb_ * 4 + i) for i in range(4)])
                        for i in range(4):
                            tt = tb_ * 4 + i
                            P.dma("sp", y_d[b, tt * 128:(tt + 1) * 128, :], x1[:, tt, :], ("x1", tt), reads=[("x1", tt)])
                else:
                    for tt in range(16):
                        P.dma("sp", y_d[b, tt * 128:(tt + 1) * 128, :], x1[:, tt, :], ("x1", tt), reads=[("x1", tt)])
    P.wait_all_dma("sp")
    P.emit()
    return nc


_NC_CACHE = {}


def _prep(inp):
    f = lambda a: np.ascontiguousarray(np.asarray(a, dtype=np.float32))
    fm = lambda v: f(np.asarray(v).reshape(-1, 128).T)
    w_in = np.asarray(inp["w_in"])[0]
    w_ada = np.asarray(inp["w_ada"])[0]
    sh = {}
    sh["w_ada"] = f(w_ada.reshape(8, 128, 6, 1024).transpose(2, 1, 0, 3))
    cw = np.asarray(inp["conv_ssm_w"])[0]
    fw = np.asarray(inp["conv_ffn_w"])[0]
    parts = {
        "b_ada": fm(np.asarray(inp["b_ada"])[0]),
        "g1": fm(np.asarray(inp["norm1_g"])[0]), "g2": fm(np.asarray(inp["norm2_g"])[0]),
        "gq": fm(np.asarray(inp["q_norm_g"])[0]), "gk": fm(np.asarray(inp["k_norm_g"])[0]),
        "cw": f(cw.reshape(4, 12, 128).transpose(2, 1, 0).reshape(128, 48)),
        "cb": fm(np.asarray(inp["conv_ssm_b"])[0]),
        "dskip": fm(np.repeat(np.asarray(inp["d_skip"])[0], 64)),
        "gssm": fm(np.asarray(inp["ssm_norm_g"])[0]), "gattn": fm(np.asarray(inp["attn_norm_g"])[0]),
        "fw": f(fw.reshape(3, 44, 128).transpose(2, 1, 0).reshape(128, 132)),
        "fb": fm(np.asarray(inp["conv_ffn_b"])[0]),
    }
    pfm = np.concatenate([parts[n] for n in _PF], axis=1)
    assert pfm.shape == (128, NPF)
    sh["pfm"] = f(pfm)
    sh["bgate"] = f(np.broadcast_to(np.asarray(inp["b_ada"])[0][2048:3072][None, :], (128, 1024)))
    sh["pbc"] = f(np.concatenate([np.broadcast_to(np.asarray(inp["dt_bias"])[0][None, :], (128, 16)),
                                  np.broadcast_to(np.asarray(inp["a_log"])[0][None, :], (128, 16))], axis=1))
    wk = w_in.reshape(8, 128, 5648)
    qkv = wk[:, :, 0:3072].reshape(8, 128, 3, 8, 128)
    sh["w_qkv"] = f(qkv.transpose(3, 1, 0, 2, 4))
    z = wk[:, :, 3072:4096].reshape(8, 128, 8, 128)
    xs = wk[:, :, 4096:5120].reshape(8, 128, 8, 128)
    sh["w_zx"] = f(np.stack([z, xs], axis=3).transpose(2, 1, 0, 3, 4))
    bc = wk[:, :, 5120:5632].reshape(8, 128, 4, 128)
    sh["w_bc"] = f(bc.transpose(1, 0, 2, 3))
    sh["w_dt"] = f(wk[:, :, 5632:5648].transpose(1, 0, 2))
    sh["w_out"] = f(np.asarray(inp["w_out"])[0].reshape(16, 128, 1024).transpose(1, 0, 2))
    wu = np.asarray(inp["w_up"])[0].reshape(8, 128, 2, NJ, 128)
    sh["w_up"] = f(wu.transpose(3, 1, 0, 2, 4))
    wd = np.asarray(inp["w_down"])[0].reshape(NJ, 128, 8, 128)
    sh["w_down"] = f(wd.transpose(2, 1, 0, 3))
    return sh


def kernel(**inp):
    x = np.asarray(inp["x"], dtype=np.float32)
    c = np.asarray(inp["c"], dtype=np.float32)
    sh = _prep(inp)
    if "nc" not in _NC_CACHE:
        _NC_CACHE["nc"] = build()
    nc = _NC_CACHE["nc"]
    in_maps = []
    for i in range(8):
        m = dict(sh)
        m["x"] = np.ascontiguousarray(x[2 * i:2 * i + 2])
        m["cT"] = np.ascontiguousarray(c[2 * i:2 * i + 2].T.reshape(8, 128, 2).transpose(1, 0, 2))
        in_maps.append(m)
    res = run_bass_kernel_spmd(nc, in_maps, core_ids=list(range(8)))
    return np.concatenate([r["y"] for r in res.results], axis=0).astype(np.float32)
```
